# Optimizing a Trainium2 kernel written in Bass

```python
import jax, jax.numpy as jnp
from jax import lax
import numpy as np

D_MODEL = 2048
BATCH = 8
SEQ = 2048
DEPTH = 1
DEC_BATCH = 1
DEC_SEQ = 16384
PAST_LEN = 128

GRID_W = 64
NA_HEADS = 16
NA_HEAD_DIM = 64
D_ATTN = NA_HEADS * NA_HEAD_DIM
WIN_ROWS_MAX = 8
WIN_COLS = 16
POOL_WINDOWS = (2, 4, 8, 16)
POOL_GROUPS = len(POOL_WINDOWS)
D_POOL = 1024
POOL_GROUP_DIM = D_POOL // POOL_GROUPS
N_BRANCH = 2
D_IN = 3 * D_ATTN + D_POOL + N_BRANCH * D_MODEL
D_FF = 5632
RMS_EPS = 1e-6

kernel_name = "hybrid_natten_pool_macaron_encoder"


def rmsnorm(x, g):
    xf = x.astype(jnp.float32)
    y = xf * lax.rsqrt(jnp.mean(xf * xf, axis=-1, keepdims=True) + RMS_EPS)
    return (y * g.astype(jnp.float32)).astype(x.dtype)


def swiglu(x, w_gate, w_up, w_down):
    return (jax.nn.silu(x @ w_gate) * (x @ w_up)) @ w_down


def neighbourhood_attention(q, k, v, rpb):
    B, T, H, dh = q.shape
    rows = T // GRID_W
    kr = min(WIN_ROWS_MAX, rows)
    qg = q.reshape(B, rows, GRID_W, H, dh)
    kg = k.reshape(B, rows, GRID_W, H, dh)
    vg = v.reshape(B, rows, GRID_W, H, dh)
    row_start = jnp.clip(jnp.arange(rows) - kr // 2, 0, rows - kr)
    col_q = jnp.arange(GRID_W)
    col_start = jnp.clip(col_q - WIN_COLS // 2, 0, GRID_W - WIN_COLS)
    col_idx = col_start[:, None] + jnp.arange(WIN_COLS)[None, :]
    dc = col_idx - col_q[:, None] + (WIN_COLS - 1)
    scale = dh ** -0.5

    def one_row(r):
        rs = row_start[r]
        q_r = lax.dynamic_index_in_dim(qg, r, axis=1, keepdims=False)
        k_band = lax.dynamic_slice_in_dim(kg, rs, kr, axis=1)
        v_band = lax.dynamic_slice_in_dim(vg, rs, kr, axis=1)
        k_win = k_band[:, :, col_idx]
        v_win = v_band[:, :, col_idx]
        s = jnp.einsum('bwhd,brwkhd->bhwrk', q_r, k_win).astype(jnp.float32) * scale
        dr = rs + jnp.arange(kr) - r + (WIN_ROWS_MAX - 1)
        bias = rpb[:, dr[None, :, None], dc[:, None, :]]
        s = s + bias.astype(jnp.float32)[None]
        p = jax.nn.softmax(s.reshape(B, H, GRID_W, kr * WIN_COLS), axis=-1)
        p = p.reshape(B, H, GRID_W, kr, WIN_COLS).astype(v.dtype)
        return jnp.einsum('bhwrk,brwkhd->bwhd', p, v_win)

    out = lax.map(one_row, jnp.arange(rows))
    return jnp.transpose(out, (1, 0, 2, 3, 4)).reshape(B, T, H * dh)


def multiscale_pool(z, w_pool, pool_scale):
    B, T, _ = z.shape
    zf = z.astype(jnp.float32)
    cs = jnp.concatenate([jnp.zeros((B, 1, D_POOL), jnp.float32), jnp.cumsum(zf, axis=1)], axis=1)
    t = jnp.arange(T)
    outs = []
    for gi, w in enumerate(POOL_WINDOWS):
        lo = jnp.clip(t - w // 2, 0, T)
        hi = jnp.clip(t + w // 2, 0, T)
        sl = slice(gi * POOL_GROUP_DIM, (gi + 1) * POOL_GROUP_DIM)
        csg = cs[:, :, sl]
        cnt = (hi - lo).astype(jnp.float32)[None, :, None]
        outs.append((csg[:, hi] - csg[:, lo]) / cnt - zf[:, :, sl])
    pooled = jnp.stack(outs, axis=2).astype(z.dtype)
    mixed = jnp.einsum('btgc,gcd->btgd', pooled, w_pool)
    return mixed.reshape(B, T, D_POOL) * pool_scale


def hybrid_mixer(u, w_in, rpb, w_pool, pool_scale, w_branch_attn, w_branch_pool, w_out):
    B, T, _ = u.shape
    proj = u @ w_in
    q, k, v, zp, g = jnp.split(proj, [D_ATTN, 2 * D_ATTN, 3 * D_ATTN, 3 * D_ATTN + D_POOL], axis=-1)
    heads = lambda a: a.reshape(B, T, NA_HEADS, NA_HEAD_DIM)
    y_attn = neighbourhood_attention(heads(q), heads(k), heads(v), rpb) @ w_branch_attn
    y_pool = multiscale_pool(zp, w_pool, pool_scale) @ w_branch_pool
    g_a, g_p = jnp.split(jax.nn.sigmoid(g), N_BRANCH, axis=-1)
    return (g_a * y_attn + g_p * y_pool) @ w_out


def encoder_trunk(x, g_ffn1, w1_gate, w1_up, w1_down, g_mix, w_in, rpb, w_pool, pool_scale,
                  w_branch_attn, w_branch_pool, w_out, g_ffn2, w2_gate, w2_up, w2_down, g_final):
    h = x
    for l in range(DEPTH):
        h = h + 0.5 * swiglu(rmsnorm(h, g_ffn1[l]), w1_gate[l], w1_up[l], w1_down[l])
        h = h + hybrid_mixer(rmsnorm(h, g_mix[l]), w_in[l], rpb[l], w_pool[l], pool_scale[l],
                             w_branch_attn[l], w_branch_pool[l], w_out[l])
        h = h + 0.5 * swiglu(rmsnorm(h, g_ffn2[l]), w2_gate[l], w2_up[l], w2_down[l])
    return rmsnorm(h, g_final)


def setup_inputs(seed: int = 0) -> dict:
    key = jax.random.key(seed)
    ks = jax.random.split(key, 20)
    f32 = jnp.float32
    nrm = lambda k, shape, s: jax.random.normal(k, shape, f32) * s
    L = DEPTH
    return {
        "x_prompt": nrm(ks[0], (BATCH, SEQ, D_MODEL), 1.0),
        "x_sample": nrm(ks[1], (DEC_BATCH, DEC_SEQ, D_MODEL), 1.0),
        "g_ffn1": 1.0 + nrm(ks[2], (L, D_MODEL), 0.02),
        "w1_gate": nrm(ks[3], (L, D_MODEL, D_FF), D_MODEL ** -0.5),
        "w1_up": nrm(ks[4], (L, D_MODEL, D_FF), D_MODEL ** -0.5),
        "w1_down": nrm(ks[5], (L, D_FF, D_MODEL), D_FF ** -0.5),
        "g_mix": 1.0 + nrm(ks[6], (L, D_MODEL), 0.02),
        "w_in": nrm(ks[7], (L, D_MODEL, D_IN), D_MODEL ** -0.5),
        "rpb": nrm(ks[8], (L, NA_HEADS, 2 * WIN_ROWS_MAX - 1, 2 * WIN_COLS - 1), 0.02),
        "w_pool": nrm(ks[9], (L, POOL_GROUPS, POOL_GROUP_DIM, POOL_GROUP_DIM), POOL_GROUP_DIM ** -0.5),
        "pool_scale": 1.0 + nrm(ks[10], (L, D_POOL), 0.02),
        "w_branch_attn": nrm(ks[11], (L, D_ATTN, D_MODEL), D_ATTN ** -0.5),
        "w_branch_pool": nrm(ks[12], (L, D_POOL, D_MODEL), D_POOL ** -0.5),
        "w_out": nrm(ks[13], (L, D_MODEL, D_MODEL), D_MODEL ** -0.5),
        "g_ffn2": 1.0 + nrm(ks[14], (L, D_MODEL), 0.02),
        "w2_gate": nrm(ks[15], (L, D_MODEL, D_FF), D_MODEL ** -0.5),
        "w2_up": nrm(ks[16], (L, D_MODEL, D_FF), D_MODEL ** -0.5),
        "w2_down": nrm(ks[17], (L, D_FF, D_MODEL), D_FF ** -0.5),
        "g_final": 1.0 + nrm(ks[18], (D_MODEL,), 0.02),
    }


def reference(x_prompt, x_sample, g_ffn1, w1_gate, w1_up, w1_down, g_mix, w_in, rpb, w_pool,
              pool_scale, w_branch_attn, w_branch_pool, w_out, g_ffn2, w2_gate, w2_up, w2_down, g_final):
    y_prompt = encoder_trunk(x_prompt, g_ffn1, w1_gate, w1_up, w1_down, g_mix, w_in, rpb, w_pool,
                             pool_scale, w_branch_attn, w_branch_pool, w_out, g_ffn2, w2_gate, w2_up,
                             w2_down, g_final)
    y_sample = encoder_trunk(x_sample, g_ffn1, w1_gate, w1_up, w1_down, g_mix, w_in, rpb, w_pool,
                             pool_scale, w_branch_attn, w_branch_pool, w_out, g_ffn2, w2_gate, w2_up,
                             w2_down, g_final)
    return (y_prompt, y_sample)
```

```python
import numpy as np
import concourse.bass as bass
import concourse.mybir as mybir
from concourse.bass_utils import run_bass_kernel_spmd

F32 = mybir.dt.float32
BF16 = mybir.dt.bfloat16
AF = mybir.ActivationFunctionType
ALU = mybir.AluOpType
AX = mybir.AxisListType

D = 2048
DFF = 5632
NJ = DFF // 128
NTOK = 4608
NOWN = 4096
NEG = -30000.0
NCORES = 8


class Res:
    __slots__ = ("name", "last_w", "readers")

    def __init__(self, name):
        self.name = name
        self.last_w = None
        self.readers = []


class Op:
    __slots__ = ("eng", "fn", "deps", "is_dma", "key", "kidx", "needed", "sig")

    def __init__(self, eng, fn, is_dma=False, key=None):
        self.eng = eng
        self.fn = fn
        self.deps = []
        self.is_dma = is_dma
        self.key = key
        self.kidx = 0
        self.needed = False
        self.sig = 0


class Prog:
    ENGS = ("pe", "act", "dve", "pool", "sp")

    def __init__(self, nc, sems):
        self.nc = nc
        self.eobj = dict(pe=nc.tensor, act=nc.scalar, dve=nc.vector, pool=nc.gpsimd, sp=nc.sync)
        self.esem = {e: sems.pop() for e in self.ENGS}
        self.free_sems = sems
        self.key_sem = {}
        self.key_cnt = {}
        self.ecnt = {e: 0 for e in self.ENGS}
        self.known = {e: {} for e in self.ENGS}
        self.ops = []
        self.last_eng_op = {e: None for e in self.ENGS}
        self.last_key_op = {}
        self.pending_barrier = None

    def _add_dep(self, op, d):
        if d is not None and d is not op:
            op.deps.append(d)

    def op(self, eng, fn, reads=(), writes=(), acc=(), dma=False, key=None):
        o = Op(eng, fn, dma, key)
        if self.pending_barrier is not None and eng not in self.pending_barrier[1]:
            for d in self.pending_barrier[0]:
                self._add_dep(o, d)
            self.pending_barrier[1].add(eng)
        for r in reads:
            self._add_dep(o, r.last_w)
        for w in writes:
            self._add_dep(o, w.last_w)
            for rd in w.readers:
                self._add_dep(o, rd)
        for r in reads:
            r.readers.append(o)
        for w in writes:
            w.last_w = o
            w.readers = []
        for w in acc:
            w.last_w = o
        if dma:
            assert key is not None
            if key not in self.key_sem:
                self.key_sem[key] = self.free_sems.pop()
                self.key_cnt[key] = 0
            self.key_cnt[key] += 1
            o.kidx = self.key_cnt[key]
            self.last_key_op[key] = o
        else:
            self.last_eng_op[eng] = o
        self.ops.append(o)
        return o

    def barrier(self):
        deps = [o for o in self.last_eng_op.values() if o is not None]
        deps += list(self.last_key_op.values())
        self.pending_barrier = (deps, set())

    def emit(self):
        for o in self.ops:
            for d in o.deps:
                d.needed = True
        for e in self.ENGS:
            if self.last_eng_op[e] is not None:
                self.last_eng_op[e].needed = True
        for o in self.ops:
            if not o.is_dma and o.needed and o.sig == 0:
                self.ecnt[o.eng] += 1
                o.sig = self.ecnt[o.eng]
        streams = {e: [] for e in self.ENGS}
        for o in self.ops:
            kn = self.known[o.eng]
            waits = []
            for d in o.deps:
                if d.is_dma:
                    s, v = self.key_sem[d.key], 16 * d.kidx
                else:
                    s, v = self.esem[d.eng], d.sig
                if kn.get(s, 0) < v:
                    kn[s] = v
                    waits.append((s, v))
            streams[o.eng].append((waits, o))
        self.ops = []
        with self.nc.Block() as block:
            for e in self.ENGS:
                items = streams[e]
                if not items:
                    continue

                def body(eng, items=items, e=e):
                    for waits, o in items:
                        best = {}
                        for s, v in waits:
                            best[s] = max(best.get(s, 0), v)
                        for s, v in best.items():
                            eng.wait_ge(s, v)
                        if o.fn is None:
                            continue
                        ins = o.fn(eng)
                        if o.is_dma:
                            ins.then_inc(self.key_sem[o.key], 16)
                        elif o.needed:
                            ins.then_inc(self.esem[e], 1)

                getattr(block, {"pe": "tensor", "act": "scalar", "dve": "vector",
                                "pool": "gpsimd", "sp": "sync"}[e])(body)

    def final_wait(self):
        items = []
        for k, s in self.key_sem.items():
            items.append((s, 16 * self.key_cnt[k]))
        with self.nc.Block() as block:
            def body(eng):
                for s, v in items:
                    eng.wait_ge(s, v)
            block.sync(body)


def _band(s, qt):
    L, own0 = (32, 0) if s == 0 else (40, 4)
    r = own0 + 2 * qt
    wide = (r >= 28) if s == 0 else (r in (4, 6, 34))
    nbr = 12 if wide else 9
    return nbr, min(max(r - 4, 0), L - nbr)


def _clipped(s, qt):
    r = (0 if s == 0 else 4) + 2 * qt
    return r in ((0, 2, 28, 30) if s == 0 else (4, 6, 32, 34))


class WStream:
    NSLOT = 6

    def __init__(self, P, ring, ring_r, sched, ngroups=1, cache=None):
        self.P, self.ring, self.ring_r, self.sched = P, ring, ring_r, sched
        self.nload = 0
        self.nget = 0
        self.free = list(range(self.NSLOT))
        self.slot_of = {}
        self.cache = cache
        assert len(sched) % ngroups == 0
        self.per_group = len(sched) // ngroups
        self.cres = [Res("wc%d" % j) for j in range(self.per_group)]

    def _pump(self):
        while self.nload < len(self.sched) and self.free:
            i = self.nload
            tag, src, nk, nc_ = self.sched[i]
            s = self.free.pop(0)
            self.slot_of[i] = s
            parts = src if isinstance(src, list) else [(src, nk, nc_)]
            ne = sum(a * b for (_, a, b) in parts)
            g, j = i // self.per_group, i % self.per_group
            rr = self.ring_r[s]
            wbg = j % 2 if self.per_group * 2 <= len(self.sched) else 0
            if self.cache is not None and g > wbg:
                self.P.op("pool", lambda e, s=s, j=j, ne=ne: e.dma_start(out=self.ring[:, s, 0:ne], in_=self.cache[j, :, 0:ne]),
                          reads=[self.cres[j]], writes=[rr], dma=True, key=rr)
            else:
                off = 0
                for (src_, nk_, ncc_) in parts:
                    dst = self.ring[:, s, off:off + nk_ * ncc_].rearrange("p (k c) -> p k c", k=nk_)
                    off += nk_ * ncc_
                    self.P.op("pool", lambda e, dst=dst, src_=src_: e.dma_start(out=dst, in_=src_),
                              writes=[rr], dma=True, key=rr)
                if self.cache is not None and g == wbg:
                    self.P.op("sp", lambda e, s=s, j=j, ne=ne: e.dma_start(out=self.cache[j, :, 0:ne], in_=self.ring[:, s, 0:ne]),
                              reads=[rr], writes=[self.cres[j]], dma=True, key=rr)
            self.nload += 1

    def get(self, tag):
        self._pump()
        i = self.nget
        t, src, nk, nc_ = self.sched[i]
        assert t == tag, (t, tag)
        assert i in self.slot_of, "weight ring exhausted"
        s = self.slot_of[i]
        self.nget += 1
        if isinstance(src, list):
            view, off = [], 0
            for (src_, nk_, ncc_) in src:
                view.append(self.ring[:, s, off:off + nk_ * ncc_].rearrange("p (k c) -> p k c", k=nk_))
                off += nk_ * ncc_
        else:
            view = self.ring[:, s, 0:nk * nc_].rearrange("p (k c) -> p k c", k=nk)
        return view, self.ring_r[s], i

    def done(self, i):
        self.free.append(self.slot_of.pop(i))
        self._pump()


def build_nc(dbg=False):
    nc = bass.Bass("TRN2", target_bir_lowering=False)
    ein = lambda n, shp: nc.dram_tensor(n, shp, F32, kind="ExternalInput").ap()
    x_all = ein("x_all", [NTOK, D])
    w1g, w1u, w1d = ein("w1_gate", [D, DFF]), ein("w1_up", [D, DFF]), ein("w1_down", [DFF, D])
    w2g, w2u, w2d = ein("w2_gate", [D, DFF]), ein("w2_up", [D, DFF]), ein("w2_down", [DFF, D])
    w_in = ein("w_in", [D, 8192])
    w_pool = ein("w_pool", [4, 256, 256])
    w_ba, w_bp = ein("w_branch_attn", [1024, D]), ein("w_branch_pool", [1024, D])
    w_out = ein("w_out", [D, D])
    gT_d = ein("gT", [128, 48])
    psT_d = ein("psT", [128, 8])
    gfin_d = ein("gfin", [128, D])
    bias_d = ein("bias_t", [8, 128, 2 * 22 * 64])
    biasi_d = ein("bias_i", [8, 128, 2 * 22 * 64])
    rm_d = ein("rowmask", [2, 12, 2048])
    e12_d = ein("e12", [12, 768])
    invc_d = ein("invcnt", [2, 4, 2048])
    y_out = nc.dram_tensor("y", [NOWN, D], F32, kind="ExternalOutput").ap()
    skind = "ExternalOutput" if dbg else "Internal"
    H1 = nc.dram_tensor("H1", [NTOK, D], F32, kind=skind).ap()
    QKV = nc.dram_tensor("QKV", [3072, NTOK], BF16, kind=skind).ap()
    ZP = nc.dram_tensor("ZP", [1024, NTOK], F32, kind=skind).ap()
    YATT = nc.dram_tensor("YATT", [1024, NOWN], BF16, kind=skind).ap()
    U2T = nc.dram_tensor("U2T", [D, NTOK], BF16, kind=skind).ap()
    WCA = nc.dram_tensor("WCA", [84, 128, 4096], BF16).ap()
    WCC = nc.dram_tensor("WCC", [101, 128, 4096], BF16).ap()

    from contextlib import ExitStack
    with ExitStack() as es:
        sems = [es.enter_context(nc.semaphore("s%d" % i)) for i in range(48)]
        P = Prog(nc, sems)
        sb = lambda n, shp, dt: es.enter_context(nc.sbuf_tensor("k_" + n, shp, dt))
        ident = sb("ident", [128, 128], BF16)
        identf = sb("identf", [128, 128], F32)
        gT = sb("gT_sb", [128, 48], F32)
        psT = sb("psT_sb", [128, 8], F32)
        st_ss = sb("st_ss", [128, 8], F32)
        st_rs = sb("st_rs", [128, 8], F32)
        pT = es.enter_context(nc.psum_tensor("pT", [128, 2, 1024], BF16))
        pF = es.enter_context(nc.psum_tensor("pF", [128, 6, 512], F32))

        R = Res
        ident_r, gT_r, psT_r = R("ident"), R("gT"), R("psT")
        xs_r = [R("xs%d" % i) for i in range(4)]
        uT_r = [R("uT%d" % i) for i in range(4)]
        un_r = [R("un%d" % i) for i in range(4)]
        ring_r = [R("ring%d" % i) for i in range(6)]
        sg_r = [R("sg%d" % i) for i in range(2)]
        ss_r = [R("ss%d" % i) for i in range(8)]
        ss4_r = [R("ss4_%d" % i) for i in range(2)]
        rs_r = [R("rs%d" % i) for i in range(8)]
        pT_r = [R("pT%d" % i) for i in range(2)]
        pF_r = [R("pF%d" % i) for i in range(6)]
        cnt = {"pT": 0, "pF": 0, "un": 0, "st": 0, "sg": 0, "ev": 0, "st4": 0}

        def nxt(k, n):
            v = cnt[k] % n
            cnt[k] += 1
            return v

        def fbank():
            b = nxt("pF", 6)
            return pF[:, b, :], pF_r[b]

        P.op("pool", lambda e: e.memset(identf[:], 0.0), writes=[ident_r])
        P.op("pool", lambda e: e.affine_select(out=identf[:], in_=identf[:], pattern=[[-1, 128]],
                                               compare_op=ALU.not_equal, fill=1.0, base=0,
                                               channel_multiplier=1), writes=[ident_r])
        P.op("dve", lambda e: e.tensor_copy(out=ident[:], in_=identf[:]), writes=[ident_r])
        P.op("sp", lambda e: e.dma_start(out=gT[:], in_=gT_d[:, :]), writes=[gT_r], dma=True, key=gT_r)
        P.op("sp", lambda e: e.dma_start(out=psT[:], in_=psT_d[:, :]), writes=[psT_r], dma=True, key=psT_r)

        def rms_stats(src, src_r):
            b = nxt("un", 4)
            s = nxt("st", 8)
            ssv, rsv = st_ss[:, s:s + 1], st_rs[:, s:s + 1]
            P.op("dve", lambda e: e.memset(ssv, 0.0), writes=[ss_r[s]])
            P.op("act", lambda e: e.activation(out=un[:, b, :], in_=src, func=AF.Square,
                                               scale=float(D ** -0.5), accum_out=ssv),
                 reads=[src_r], writes=[un_r[b], ss_r[s]])
            P.op("dve", lambda e: e.tensor_scalar_add(out=ssv, in0=ssv, scalar1=1e-6),
                 reads=[ss_r[s]], writes=[ss_r[s]])
            P.op("act", lambda e: e.activation(out=rsv, in_=ssv, func=AF.Sqrt),
                 reads=[ss_r[s]], writes=[rs_r[s]])
            P.op("dve", lambda e: e.reciprocal(out=rsv, in_=rsv), reads=[rs_r[s]], writes=[rs_r[s]])
            return rsv, rs_r[s], b

        def norm_transpose(src, src_r, gidx, t):
            rsv, rs_res, b = rms_stats(src, src_r)
            P.op("act", lambda e: e.activation(out=un[:, b, :], in_=src, func=AF.Copy, scale=rsv),
                 reads=[src_r, rs_res], writes=[un_r[b]])
            for h in range(2):
                pb = nxt("pT", 2)

                def tr(e, h=h, pb=pb):
                    for kk in range(8):
                        k = h * 8 + kk
                        i = e.transpose(out=pT[:, pb, kk * 128:(kk + 1) * 128],
                                        in_=un[:, b, k * 128:(k + 1) * 128], identity=ident[:])
                    return i
                P.op("pe", tr, reads=[un_r[b], ident_r], writes=[pT_r[pb]])
                gsl = gT[:, gidx * 16 + h * 8: gidx * 16 + h * 8 + 8]
                P.op("dve", lambda e, h=h, pb=pb, gsl=gsl: e.tensor_tensor(
                    out=uT[:, h * 8:(h + 1) * 8, t * 128:(t + 1) * 128],
                    in0=pT[:, pb, :].rearrange("p (k c) -> p k c", k=8),
                    in1=gsl.unsqueeze(2).to_broadcast([128, 8, 128]), op=ALU.mult),
                    reads=[pT_r[pb], gT_r], writes=[uT_r[t]])

        def norm_scale(tiles):
            g4 = nxt("st4", 2)
            ss4, rs4 = st_ss[:, g4 * 4:g4 * 4 + 4], st_rs[:, g4 * 4:g4 * 4 + 4]
            res4 = ss4_r[g4]
            P.op("dve", lambda e: e.memset(ss4, 0.0), writes=[res4])
            for t, (src, src_r) in enumerate(tiles):
                P.op("act", lambda e, t=t, src=src: e.activation(out=un[:, t, :], in_=src, func=AF.Square,
                                                                 scale=float(D ** -0.5), accum_out=ss4[:, t:t + 1]),
                     reads=[src_r, res4], writes=[un_r[t], res4])
            P.op("dve", lambda e: e.tensor_scalar_add(out=ss4, in0=ss4, scalar1=1e-6), reads=[res4], writes=[res4])
            P.op("act", lambda e: e.activation(out=rs4, in_=ss4, func=AF.Sqrt), reads=[res4], writes=[res4])
            P.op("dve", lambda e: e.reciprocal(out=rs4, in_=rs4), reads=[res4], writes=[res4])
            for t, (src, src_r) in enumerate(tiles):
                if t % 2 == 0:
                    P.op("act", lambda e, t=t, src=src: e.activation(out=un[:, t, :], in_=src, func=AF.Copy,
                                                                     scale=rs4[:, t:t + 1]),
                         reads=[src_r, res4], writes=[un_r[t]])
                else:
                    P.op("dve", lambda e, t=t, src=src: e.tensor_scalar(out=un[:, t, :], in0=src, scalar1=rs4[:, t:t + 1],
                                                                        scalar2=None, op0=ALU.mult),
                         reads=[src_r, res4], writes=[un_r[t]])

        def norm_tr(gidx, uTd, uTd_r):
            for t in range(4):
                for h in range(2):
                    pb = nxt("pT", 2)

                    def tr(e, h=h, pb=pb, t=t):
                        for kk in range(8):
                            k = h * 8 + kk
                            i = e.transpose(out=pT[:, pb, kk * 128:(kk + 1) * 128],
                                            in_=un[:, t, k * 128:(k + 1) * 128], identity=ident[:])
                        return i
                    P.op("pe", tr, reads=[un_r[t], ident_r], writes=[pT_r[pb]])
                    gsl = gT[:, gidx * 16 + h * 8: gidx * 16 + h * 8 + 8]
                    P.op("dve", lambda e, h=h, pb=pb, gsl=gsl, t=t: e.tensor_tensor(
                        out=uTd[:, h * 8:(h + 1) * 8, t * 128:(t + 1) * 128],
                        in0=pT[:, pb, :].rearrange("p (k c) -> p k c", k=8),
                        in1=gsl.unsqueeze(2).to_broadcast([128, 8, 128]), op=ALU.mult),
                        reads=[pT_r[pb], gT_r], writes=[uTd_r[t]])

        def norm_group(tiles, gidx, uTd, uTd_r):
            norm_scale(tiles)
            norm_tr(gidx, uTd, uTd_r)

        DQ = [(0, 8), (8, 8), (16, 8), (24, 8), (32, 8), (40, 4)]

        def ffn_sched(wg, wu, wd, name):
            s = []
            wgv = wg.rearrange("(k p) c -> p k c", p=128)
            wuv = wu.rearrange("(k p) c -> p k c", p=128)
            wdv = wd.rearrange("(j p) c -> p j c", p=128)
            for jb in range(22):
                s.append((name + "g%d" % jb, wgv[:, :, jb * 256:(jb + 1) * 256], 16, 256))
                s.append((name + "u%d" % jb, wuv[:, :, jb * 256:(jb + 1) * 256], 16, 256))
            for c in range(4):
                for qi, (q0, qn) in enumerate(DQ):
                    s.append((name + "d%d_%d" % (c, qi), wdv[:, q0:q0 + qn, c * 512:(c + 1) * 512], qn, 512))
            return s

        def ffn(ws, name, aT, aT_r, hs_tiles, uT=None, uT_r=None, mid_hook=None):
            uT, uT_r = (uT_main, uT_main_r) if uT is None else (uT, uT_r)
            for jb in range(22):
                wgs, wg_r, ig = ws.get(name + "g%d" % jb)
                wus, wu_r, iu = ws.get(name + "u%d" % jb)
                for j in range(2):
                    jj = jb * 2 + j
                    pg, pg_r = fbank()
                    pu, pu_r = fbank()

                    def mmg(e, w=wgs, o=pg, j=j):
                        for k in range(16):
                            i = e.matmul(o, lhsT=w[:, k, j * 128:(j + 1) * 128], rhs=uT[:, k, :],
                                         start=(k == 0), stop=(k == 15))
                        return i
                    P.op("pe", mmg, reads=uT_r + [wg_r], writes=[pg_r])
                    P.op("pe", lambda e, w=wus, o=pu, j=j, f=mmg: f(e, w, o, j), reads=uT_r + [wu_r], writes=[pu_r])
                    sb_ = nxt("sg", 2)
                    P.op("act", lambda e, sb_=sb_, pg=pg: e.activation(out=sg[:, sb_, :], in_=pg, func=AF.Silu),
                         reads=[pg_r], writes=[sg_r[sb_]])
                    P.op("dve", lambda e, sb_=sb_, pu=pu, jj=jj: e.tensor_tensor(
                        out=aT[:, jj, :], in0=sg[:, sb_, :], in1=pu, op=ALU.mult),
                        reads=[sg_r[sb_], pu_r], writes=[aT_r[jj]])
                ws.done(ig)
                ws.done(iu)
            if mid_hook is not None:
                mid_hook()
            for c in range(4):
                banks = [fbank() for _ in range(4)]
                for qi, (q0, qn) in enumerate(DQ):
                    wds, wd_r, idd = ws.get(name + "d%d_%d" % (c, qi))
                    for t in range(4):
                        pd, pd_r = banks[t]

                        def mmd(e, w=wds, o=pd, q0=q0, qn=qn, t=t):
                            for j in range(qn):
                                i = e.matmul(o, lhsT=aT[:, q0 + j, t * 128:(t + 1) * 128], rhs=w[:, j, :],
                                             start=(q0 + j == 0), stop=(q0 + j == NJ - 1))
                            return i
                        rd = aT_r[q0:q0 + qn] + [wd_r]
                        if qi == 0:
                            P.op("pe", mmd, reads=rd, writes=[pd_r])
                        else:
                            P.op("pe", mmd, reads=rd, acc=[pd_r])
                    ws.done(idd)
                for t in range(4):
                    pd, pd_r = banks[t]
                    hv, hr = hs_tiles[t]
                    P.op("dve", lambda e, pd=pd, hv=hv, c=c: e.scalar_tensor_tensor(
                        out=hv[:, c * 512:(c + 1) * 512], in0=pd, scalar=0.5, in1=hv[:, c * 512:(c + 1) * 512],
                        op0=ALU.mult, op1=ALU.add), reads=[pd_r, hr], writes=[hr])

        w_in_v = w_in.rearrange("(k p) c -> p k c", p=128)

        with ExitStack() as pa:
            xs = pa.enter_context(nc.sbuf_tensor("a_xs", [128, 4, D], F32))
            uT = pa.enter_context(nc.sbuf_tensor("a_uT", [128, 16, 512], BF16))
            un = pa.enter_context(nc.sbuf_tensor("a_un", [128, 4, D], BF16))
            ring = pa.enter_context(nc.sbuf_tensor("a_ring", [128, 6, 4096], BF16))
            sg = pa.enter_context(nc.sbuf_tensor("a_sg", [128, 2, 512], F32))
            hs_tiles = [(xs[:, t, :], xs_r[t]) for t in range(4)]
            uT_main, uT_main_r = uT, uT_r
            aT = pa.enter_context(nc.sbuf_tensor("aT_a", [128, NJ, 512], BF16))
            stgb = pa.enter_context(nc.sbuf_tensor("stgb", [128, 2, 4, 512], BF16))
            stgf = pa.enter_context(nc.sbuf_tensor("stgf", [128, 2, 4, 512], F32))
            aT_r = [R("aT%d" % j) for j in range(NJ)]
            stgb_r = [R("stgb%d" % i) for i in range(2)]
            stgf_r = [R("stgf%d" % i) for i in range(2)]
            NGA = NTOK // 512
            sched = []
            for gi in range(NGA):
                sched += ffn_sched(w1g, w1u, w1d, "a%d" % gi)
                for nb in range(16):
                    sched.append(("a%din%d" % (gi, nb), w_in_v[:, :, nb * 256:(nb + 1) * 256], 16, 256))
            ws = WStream(P, ring, ring_r, sched, ngroups=NGA, cache=WCA)
            uT2 = pa.enter_context(nc.sbuf_tensor("k_uT2", [128, 16, 512], BF16))
            uT2_r = [R("uT2_%d" % i) for i in range(4)]

            def load_x(gi):
                for t in range(4):
                    r0 = gi * 512 + t * 128
                    P.op("sp", lambda e, t=t, r0=r0: e.dma_start(out=xs[:, t, :], in_=x_all[r0:r0 + 128, :]),
                         writes=[xs_r[t]], dma=True, key=xs_r[t])
            load_x(0)
            norm_group(hs_tiles, 0, uT, uT_r)
            for gi in range(NGA):
                tok0 = gi * 512
                ffn(ws, "a%d" % gi, aT, aT_r, hs_tiles)
                for t in range(4):
                    r0 = tok0 + t * 128
                    P.op("sp", lambda e, t=t, r0=r0: e.dma_start(out=H1[r0:r0 + 128, :], in_=xs[:, t, :]),
                         reads=[xs_r[t]], dma=True, key=xs_r[t])
                norm_group(hs_tiles, 1, uT2, uT2_r)
                P.op("sp", lambda e, tok0=tok0: e.dma_start(
                    out=U2T[:, tok0:tok0 + 512].rearrange("(k p) t -> p k t", p=128), in_=uT2[:]),
                    reads=uT2_r, dma=True, key=uT2_r[0])
                if gi + 1 < NGA:
                    load_x(gi + 1)
                for nb in range(8):
                    sp_ = nb % 2
                    isz = nb >= 6
                    stg, stg_res = (stgf, stgf_r[sp_]) if isz else (stgb, stgb_r[sp_])
                    for half in range(2):
                        wv, w_r, iw = ws.get("a%din%d" % (gi, nb * 2 + half))
                        for j2 in range(2):
                            j = half * 2 + j2
                            po, po_r = fbank()

                            def mmi(e, w=wv, o=po, j2=j2):
                                for k in range(16):
                                    i = e.matmul(o, lhsT=w[:, k, j2 * 128:(j2 + 1) * 128], rhs=uT2[:, k, :],
                                                 start=(k == 0), stop=(k == 15))
                                return i
                            P.op("pe", mmi, reads=uT2_r + [w_r], writes=[po_r])
                            dst = stg[:, sp_, j, :]
                            if nb < 2:
                                P.op("act", lambda e, dst=dst, po=po: e.activation(out=dst, in_=po, func=AF.Copy, scale=0.125),
                                     reads=[po_r], writes=[stg_res])
                            elif nxt("ev", 2) == 0:
                                P.op("act", lambda e, dst=dst, po=po: e.activation(out=dst, in_=po, func=AF.Copy),
                                     reads=[po_r], writes=[stg_res])
                            else:
                                P.op("dve", lambda e, dst=dst, po=po: e.tensor_copy(out=dst, in_=po),
                                     reads=[po_r], writes=[stg_res])
                        ws.done(iw)
                    if isz:
                        dstd = ZP[(nb - 6) * 512:(nb - 5) * 512, tok0:tok0 + 512].rearrange("(j p) t -> p j t", p=128)
                    else:
                        dstd = QKV[nb * 512:(nb + 1) * 512, tok0:tok0 + 512].rearrange("(j p) t -> p j t", p=128)
                    P.op("sp", lambda e, dstd=dstd, stg=stg, sp_=sp_: e.dma_start(out=dstd, in_=stg[:, sp_]),
                         reads=[stg_res], dma=True, key=stg_res)
                    if nb == 1 and gi + 1 < NGA:
                        norm_scale(hs_tiles)
                    if nb == 4 and gi + 1 < NGA:
                        norm_tr(0, uT, uT_r)
            P.barrier()
            P.emit()

        with ExitStack() as pb_:
            sbb = lambda n, shp, dt: pb_.enter_context(nc.sbuf_tensor("b_" + n, shp, dt))
            NSB = 3
            qz = sbb("qz", [128, 2, 2, 2048], BF16)
            kT = sbb("kT", [128, 2, 2560], BF16)
            vT = sbb("vT", [128, 2, 2560], BF16)
            Vp = sbb("Vp", [128, 2, 20, 2, 128], BF16)
            T2 = sbb("T2", [128, 2, 2, 22, 64], F32)
            T2i = sbb("T2i", [128, 2, 2, 22, 64], F32)
            rmf = sbb("rmf", [12, 2048], F32)
            rmT = sbb("rmT", [12, 2, 2048], BF16)
            e12 = sbb("e12", [12, 768], BF16)
            Sb = sbb("Sb", [128, NSB, 768], F32)
            Pf = sbb("Pf", [128, NSB, 768], F32)
            Pn = sbb("Pn", [128, NSB, 768], BF16)
            PT = sbb("PT", [128, 4, 6, 128], BF16)
            yT = sbb("yT", [128, 2, 2048], BF16)
            mx = sbb("mx", [128, 8], F32)
            sm = sbb("sm", [128, 8], F32)
            qT_r = [R("qT%d" % i) for i in range(2)]
            kT_r = [R("kT%d" % i) for i in range(2)]
            vT_r = [R("vT%d" % i) for i in range(2)]
            Vp_r = [R("Vp%d" % i) for i in range(2)]
            T2_r = [R("T2%d" % i) for i in range(2)]
            rm_r, rmf_r, e12_r = R("rm"), R("rmf"), R("e12")
            Sb_r = [R("Sb%d" % i) for i in range(NSB)]
            Pf_r = [R("Pf%d" % i) for i in range(NSB)]
            Pn_r = [R("Pn%d" % i) for i in range(NSB)]
            PT_r = [R("PT%d" % i) for i in range(4)]
            yT_r = [R("yT%d" % i) for i in range(2)]
            mx_r = [R("mx%d" % i) for i in range(8)]
            sm_r = [R("sm%d" % i) for i in range(8)]
            for s in range(2):
                P.op("sp", lambda e, s=s: e.dma_start(out=rmf[:], in_=rm_d[s]), writes=[rmf_r], dma=True, key=rmf_r)
                P.op("dve", lambda e, s=s: e.tensor_copy(out=rmT[:, s, :], in_=rmf[:]), reads=[rmf_r], writes=[rm_r])
            P.op("sp", lambda e: e.dma_start(out=rmf[:, 0:768], in_=e12_d[:, :]), writes=[rmf_r], dma=True, key=rmf_r)
            P.op("dve", lambda e: e.tensor_copy(out=e12[:], in_=rmf[:, 0:768]), reads=[rmf_r], writes=[e12_r])
            P.op("pool", lambda e: e.memset(Vp[:], 0.0), writes=Vp_r)
            P.op("pool", lambda e: e.memset(qz[:], 0.0), writes=qT_r)

            SEQ = [(32, 0, 0), (40, 2048, 4)]
            NBLK = 16
            NIT = 32

            def b_load(bi):
                s, hp, hb = bi // 8, bi % 8, bi % 2
                L, tokb, own0 = SEQ[s]
                q0 = tokb + own0 * 64
                for eh in range(2):
                    P.op("sp", lambda e, eh=eh: e.dma_start(
                        out=qz[eh * 64:(eh + 1) * 64, hb, eh, :],
                        in_=QKV[hp * 128 + eh * 64:hp * 128 + (eh + 1) * 64, q0:q0 + 2048]),
                        writes=[qT_r[hb]], dma=True, key=qT_r[hb])
                P.op("sp", lambda e: e.dma_start(
                    out=kT[:, hb, 0:L * 64], in_=QKV[1024 + hp * 128:1024 + (hp + 1) * 128, tokb:tokb + L * 64]),
                    writes=[kT_r[hb]], dma=True, key=kT_r[hb])
                P.op("sp", lambda e: e.dma_start(
                    out=vT[:, hb, 0:L * 64], in_=QKV[2048 + hp * 128:2048 + (hp + 1) * 128, tokb:tokb + L * 64]),
                    writes=[vT_r[hb]], dma=True, key=vT_r[hb])
                P.op("sp", lambda e: e.dma_start(out=T2[:, hb], in_=bias_d[hp].rearrange("p (e i c) -> p e i c", e=2, i=22)),
                     writes=[T2_r[hb]], dma=True, key=T2_r[hb])
                P.op("sp", lambda e: e.dma_start(out=T2i[:, hb], in_=biasi_d[hp].rearrange("p (e i c) -> p e i c", e=2, i=22)),
                     writes=[T2_r[hb]], dma=True, key=T2_r[hb])

            def b_vtrans(bi, pb):
                s, hb = bi // 8, bi % 2
                nkt = SEQ[s][0] // 2
                for k0 in range(0, nkt, 8):
                    n = min(8, nkt - k0)

                    def trv(e, k0=k0, n=n, pb=pb):
                        for kk in range(n):
                            i = e.transpose(out=pT[:, pb, kk * 128:(kk + 1) * 128],
                                            in_=vT[:, hb, (k0 + kk) * 128:(k0 + kk + 1) * 128], identity=ident[:])
                        return i
                    P.op("pe", trv, reads=[vT_r[hb], ident_r], writes=[pT_r[pb]])
                    pv = pT[:, pb, 0:n * 128].rearrange("p (k c) -> p k c", k=n)
                    P.op("dve", lambda e, k0=k0, n=n, pv=pv: e.tensor_copy(out=Vp[:, hb, k0:k0 + n, 0, 0:64], in_=pv[:, :, 0:64]),
                         reads=[pT_r[pb]], writes=[Vp_r[hb]])
                    P.op("act", lambda e, k0=k0, n=n, pv=pv: e.activation(out=Vp[:, hb, k0:k0 + n, 1, 64:128], in_=pv[:, :, 64:128], func=AF.Copy),
                         reads=[pT_r[pb]], writes=[Vp_r[hb]])

            def geom(n):
                bi, loc = n // NIT, n % NIT
                s, hb = bi // 8, bi % 2
                L, tokb, own0 = SEQ[s]
                qt, eh = loc // 2, loc % 2
                r = own0 + 2 * qt
                nbr, kb = _band(s, qt)
                return dict(s=s, hb=hb, qt=qt, eh=eh, kb=kb, nbr=nbr, i0=kb - r + 10, kt0=kb // 2, clip=_clipped(s, qt),
                            ncol=nbr * 64, nT=(nbr + 1) // 2)

            def stage_S(n):
                g = geom(n)
                s, hb, qt, kb, nbr = g["s"], g["hb"], g["qt"], g["kb"], g["nbr"]
                lo, hi = g["eh"] * 64, g["eh"] * 64 + 64
                ba, bb = (n % 2) * 2, (n % 2) * 2 + 1
                pa_, pbk = pF[:, ba, :], pF[:, bb, :]
                nB = g["ncol"] - 512

                eh, clip = g["eh"], g["clip"]

                def mms(e):
                    ql = qz[:, hb, eh, qt * 128:(qt + 1) * 128]
                    e.matmul(pa_, lhsT=ql, rhs=kT[:, hb, kb * 64:kb * 64 + 512], start=True, stop=not clip)
                    if clip:
                        e.matmul(pa_, lhsT=rmT[0:nbr, s, qt * 128:(qt + 1) * 128], rhs=e12[0:nbr, 0:512],
                                 start=False, stop=True)
                    i = e.matmul(pbk[:, 0:nB], lhsT=ql, rhs=kT[:, hb, kb * 64 + 512:kb * 64 + 512 + nB],
                                 start=True, stop=not clip)
                    if clip:
                        i = e.matmul(pbk[:, 0:nB], lhsT=rmT[0:nbr, s, qt * 128:(qt + 1) * 128], rhs=e12[0:nbr, 512:512 + nB],
                                     start=False, stop=True)
                    return i
                P.op("pe", mms, reads=[qT_r[hb], kT_r[hb], rm_r, e12_r], writes=[pF_r[ba], pF_r[bb]])

            def stage_bias(n):
                g = geom(n)
                hb, eh, i0, nbr, ncol = g["hb"], g["eh"], g["i0"], g["nbr"], g["ncol"]
                ba, bb = (n % 2) * 2, (n % 2) * 2 + 1
                pa_, pbk = pF[:, ba, :], pF[:, bb, :]
                sbi, si = n % NSB, n % 8
                mxv, smv = mx[:, si:si + 1], sm[:, si:si + 1]
                TT = T2 if g["clip"] else T2i
                P.op("dve", lambda e: e.tensor_tensor(
                    out=Sb[:, sbi, 0:512], in0=pa_, in1=TT[:, hb, eh, i0:i0 + 8, :].rearrange("p i c -> p (i c)"),
                    op=ALU.add), reads=[pF_r[ba], T2_r[hb]], writes=[Sb_r[sbi]])
                P.op("dve", lambda e: e.tensor_tensor(
                    out=Sb[:, sbi, 512:ncol], in0=pbk[:, 0:ncol - 512],
                    in1=TT[:, hb, eh, i0 + 8:i0 + nbr, :].rearrange("p i c -> p (i c)"),
                    op=ALU.add), reads=[pF_r[bb], T2_r[hb], Sb_r[sbi]], writes=[Sb_r[sbi]])
                P.op("dve", lambda e: e.reduce_max(out=mxv, in_=Sb[:, sbi, 0:ncol], axis=AX.X),
                     reads=[Sb_r[sbi]], writes=[mx_r[si]])
                P.op("dve", lambda e: e.tensor_scalar_mul(out=mxv, in0=mxv, scalar1=-1.0),
                     reads=[mx_r[si]], writes=[mx_r[si]])
                P.op("dve", lambda e: e.memset(smv, 0.0), writes=[sm_r[si]])

            def stage_exp(n):
                ncol = geom(n)["ncol"]
                sbi, si = n % NSB, n % 8
                mxv, smv = mx[:, si:si + 1], sm[:, si:si + 1]
                P.op("act", lambda e: e.activation(
                    out=Pf[:, sbi, 0:ncol], in_=Sb[:, sbi, 0:ncol], func=AF.Exp, bias=mxv, scale=1.0, accum_out=smv),
                    reads=[Sb_r[sbi], mx_r[si]], writes=[Pf_r[sbi], sm_r[si]])

            def stage_recip(n):
                si = n % 8
                smv = sm[:, si:si + 1]
                P.op("dve", lambda e: e.reciprocal(out=smv, in_=smv), reads=[sm_r[si]], writes=[sm_r[si]])

            def stage_norm(n):
                ncol = geom(n)["ncol"]
                sbi, si = n % NSB, n % 8
                smv = sm[:, si:si + 1]
                P.op("act", lambda e: e.activation(
                    out=Pn[:, sbi, 0:ncol], in_=Pf[:, sbi, 0:ncol], func=AF.Copy, scale=smv),
                    reads=[Pf_r[sbi], sm_r[si]], writes=[Pn_r[sbi]])

            def stage_T(n):
                g = geom(n)
                nT, ncol = g["nT"], g["ncol"]
                sbi, pb = n % NSB, n % 2

                def trp(e):
                    for kk in range(nT):
                        w = min(128, ncol - kk * 128)
                        i = e.transpose(out=pT[0:w, pb, kk * 128:(kk + 1) * 128],
                                        in_=Pn[:, sbi, kk * 128:kk * 128 + w], identity=ident[:])
                    return i
                P.op("pe", trp, reads=[Pn_r[sbi], ident_r], writes=[pT_r[pb]])

            def stage_copy(n):
                nT = geom(n)["nT"]
                pb, pti = n % 2, n % 4
                P.op("act", lambda e: e.activation(
                    out=PT[:, pti, 0:nT, :], in_=pT[:, pb, 0:nT * 128].rearrange("p (k c) -> p k c", k=nT), func=AF.Copy),
                    reads=[pT_r[pb]], writes=[PT_r[pti]])

            def stage_PV(n):
                g = geom(n)
                hb, qt, kt0, nT, ncol = g["hb"], g["qt"], g["kt0"], g["nT"], g["ncol"]
                yb_ = 4 + (qt % 2)
                py = pF[:, yb_, :]

                def mmy(e):
                    for eh_ in range(2):
                        pti = (n - 1 + eh_) % 4
                        for kk in range(nT):
                            w = min(128, ncol - kk * 128)
                            i = e.matmul(py[:, 0:128], lhsT=Vp[0:w, hb, kt0 + kk, eh_, :], rhs=PT[0:w, pti, kk, :],
                                         start=(eh_ == 0 and kk == 0), stop=(eh_ == 1 and kk == nT - 1))
                    return i
                P.op("pe", mmy, reads=[Vp_r[hb], PT_r[(n - 1) % 4], PT_r[n % 4]], writes=[pF_r[yb_]])

            def stage_y(n):
                g = geom(n)
                hb, qt = g["hb"], g["qt"]
                yb_ = 4 + (qt % 2)
                py = pF[:, yb_, :]
                P.op("dve", lambda e: e.tensor_copy(out=yT[:, hb, qt * 128:(qt + 1) * 128], in_=py[:, 0:128]),
                     reads=[pF_r[yb_]], writes=[yT_r[hb]])
                if qt == 15:
                    bi = n // NIT
                    s, hp = bi // 8, bi % 8
                    P.op("sp", lambda e: e.dma_start(
                        out=YATT[hp * 128:(hp + 1) * 128, s * 2048:(s + 1) * 2048], in_=yT[:, hb, :]),
                        reads=[yT_r[hb]], dma=True, key=yT_r[hb])

            b_load(0)
            b_vtrans(0, 0)
            NTOT = NBLK * NIT
            for step in range(NTOT + 8):
                if step < NTOT and step % NIT == 8 and step // NIT + 1 < NBLK:
                    b_load(step // NIT + 1)
                if step < NTOT:
                    stage_S(step)
                if 0 <= step - 3 < NTOT:
                    stage_recip(step - 3)
                if 0 <= step - 1 < NTOT:
                    stage_bias(step - 1)
                if 0 <= step - 2 < NTOT:
                    stage_exp(step - 2)
                if 0 <= step - 3 < NTOT:
                    stage_norm(step - 3)
                if 0 <= step - 4 < NTOT:
                    stage_T(step - 4)
                if 0 <= step - 5 < NTOT:
                    stage_copy(step - 5)
                if 0 <= step - 6 < NTOT and (step - 6) % 2 == 1:
                    stage_PV(step - 6)
                if 0 <= step - 7 < NTOT and (step - 7) % 2 == 1:
                    stage_y(step - 7)
                if step < NTOT and step % NIT == 16 and step // NIT + 1 < NBLK:
                    b_vtrans(step // NIT + 1, (step - 5) % 2)
            P.barrier()
            P.emit()

        with ExitStack() as pc:
            xs = pc.enter_context(nc.sbuf_tensor("c_xs", [128, 4, D], F32))
            uT = pc.enter_context(nc.sbuf_tensor("c_uT", [128, 16, 512], BF16))
            un = pc.enter_context(nc.sbuf_tensor("c_un", [128, 4, D], BF16))
            ring = pc.enter_context(nc.sbuf_tensor("c_ring", [128, 6, 4096], BF16))
            sg = pc.enter_context(nc.sbuf_tensor("c_sg", [128, 2, 512], F32))
            hs_tiles = [(xs[:, t, :], xs_r[t]) for t in range(4)]
            uT_main, uT_main_r = uT, uT_r
            sbc = lambda n, shp, dt: pc.enter_context(nc.sbuf_tensor("c_" + n, shp, dt))
            aT = sbc("aT_c", [128, NJ, 512], BF16)
            aT_r = [R("aTc%d" % j) for j in range(NJ)]
            zp = sbc("zp", [128, 8, 528], F32)
            tA = sbc("tA", [128, 2, 528], F32)
            tB = sbc("tB", [128, 2, 528], F32)
            invc = sbc("invc", [128, 4, 512], F32)
            gfin = sbc("gfin", [128, D], F32)
            t1 = sbc("t1", [128, 2, 512], F32)
            zp_r, tA_r, tB_r, invc_r, gfin_r = R("zp"), R("tA"), R("tB"), R("invc"), R("gfin")
            t1_r = [R("t1_%d" % i) for i in range(2)]
            mT, mT_r = aT[:, 0:16], aT_r[0:16]
            yaT, yaT_r = aT[:, 16:24], aT_r[16:24]
            pmT, pmT_r = aT[:, 24:32], aT_r[24:32]
            pl, pl_r = aT[:, 32:40], aT_r[32:40]
            P.op("sp", lambda e: e.dma_start(out=gfin[:], in_=gfin_d[:, :]), writes=[gfin_r], dma=True, key=gfin_r)
            wpv = w_pool.rearrange("g (k p) d -> p (g k) d", p=128)
            wbav = w_ba.rearrange("(k p) c -> p k c", p=128)
            wbpv = w_bp.rearrange("(k p) c -> p k c", p=128)
            wov = w_out.rearrange("(k p) c -> p k c", p=128)
            sched = []
            for gi in range(8):
                for mb in range(4):
                    for hf in range(2):
                        c0 = mb * 512 + hf * 256
                        sched.append(("c%dga%d_%d" % (gi, mb, hf), w_in_v[:, :, 4096 + c0:4096 + c0 + 256], 16, 256))
                        sched.append(("c%dgp%d_%d" % (gi, mb, hf), w_in_v[:, :, 6144 + c0:6144 + c0 + 256], 16, 256))
                        if mb == 0 and hf == 0:
                            sched.append(("c%dpool" % gi, wpv, 8, 256))
                        sched.append(("c%dbb%d_%d" % (gi, mb, hf),
                                      [(wbav[:, :, c0:c0 + 256], 8, 256), (wbpv[:, :, c0:c0 + 256], 8, 256)], None, None))
                for c in range(8):
                    sched.append(("c%dwo%d" % (gi, c), wov[:, :, c * 256:(c + 1) * 256], 16, 256))
                sched += ffn_sched(w2g, w2u, w2d, "c%d" % gi)
            ws = WStream(P, ring, ring_r, sched, ngroups=8, cache=WCC)

            def c_tok0(gi):
                return (0 if gi < 4 else 2048 + 256) + (gi % 4) * 512

            def load_u(gi):
                t0 = c_tok0(gi)
                P.op("sp", lambda e: e.dma_start(out=uT[:], in_=U2T[:, t0:t0 + 512].rearrange("(k p) t -> p k t", p=128)),
                     writes=uT_r, dma=True, key=uT_r[0])
            def c_geom(gi):
                return gi // 4, (gi % 4) * 512, c_tok0(gi)

            def mixer_loads_early(gi):
                s, o, tok0 = c_geom(gi)
                lo_c, hi_c = tok0 - 8, tok0 + 520
                if s == 0:
                    lo_v, hi_v = max(lo_c, 0), min(hi_c, 2048)
                else:
                    lo_v, hi_v = lo_c, hi_c
                if lo_v > lo_c:
                    P.op("pool", lambda e, n=lo_v - lo_c: e.memset(zp[:, :, 0:n], 0.0), writes=[zp_r])
                if hi_v < hi_c:
                    P.op("pool", lambda e, n=hi_c - hi_v: e.memset(zp[:, :, 528 - n:528], 0.0), writes=[zp_r])
                P.op("sp", lambda e, lo_v=lo_v, hi_v=hi_v, lo_c=lo_c: e.dma_start(
                    out=zp[:, :, lo_v - lo_c:hi_v - lo_c], in_=ZP[:, lo_v:hi_v].rearrange("(c p) t -> p c t", p=128)),
                    writes=[zp_r], dma=True, key=zp_r)
                P.op("sp", lambda e, s=s, o=o: e.dma_start(
                    out=invc[:], in_=invc_d[s:s + 1, :, o:o + 512].to_broadcast([128, 4, 512])),
                    writes=[invc_r], dma=True, key=invc_r)

            def mixer_pre(gi):
                s, o, tok0 = c_geom(gi)
                P.op("sp", lambda e, s=s, o=o: e.dma_start(
                    out=yaT, in_=YATT[:, s * 2048 + o:s * 2048 + o + 512].rearrange("(c p) t -> p c t", p=128)),
                    writes=list(yaT_r), dma=True, key=yaT_r[0])
                for gp in range(4):
                    zz = zp[:, 2 * gp:2 * gp + 2, :]
                    add = lambda e, o_, a, b: e.tensor_tensor(out=o_, in0=a, in1=b, op=ALU.add)
                    if gp == 0:
                        P.op("dve", lambda e, zz=zz: add(e, tB[:, :, 8:520], zz[:, :, 7:519], zz[:, :, 8:520]),
                             reads=[zp_r], writes=[tB_r])
                        wsum = tB
                    else:
                        P.op("dve", lambda e, zz=zz: add(e, tA[:, :, 1:527], zz[:, :, 0:526], zz[:, :, 1:527]),
                             reads=[zp_r], writes=[tA_r])
                        if gp == 1:
                            P.op("dve", lambda e: add(e, tB[:, :, 8:520], tA[:, :, 7:519], tA[:, :, 9:521]),
                                 reads=[tA_r], writes=[tB_r])
                            wsum = tB
                        else:
                            P.op("dve", lambda e: add(e, tB[:, :, 2:526], tA[:, :, 1:525], tA[:, :, 3:527]),
                                 reads=[tA_r], writes=[tB_r])
                            if gp == 2:
                                P.op("dve", lambda e: add(e, tA[:, :, 8:520], tB[:, :, 6:518], tB[:, :, 10:522]),
                                     reads=[tB_r], writes=[tA_r])
                                wsum = tA
                            else:
                                P.op("dve", lambda e: add(e, tA[:, :, 4:524], tB[:, :, 2:522], tB[:, :, 6:526]),
                                     reads=[tB_r], writes=[tA_r])
                                P.op("dve", lambda e: add(e, tB[:, :, 8:520], tA[:, :, 4:516], tA[:, :, 12:524]),
                                     reads=[tA_r], writes=[tB_r])
                                wsum = tB
                    w_res = tB_r if wsum is tB else tA_r
                    P.op("dve", lambda e, wsum=wsum, gp=gp: e.tensor_tensor(
                        out=wsum[:, :, 8:520], in0=wsum[:, :, 8:520],
                        in1=invc[:, gp:gp + 1, :].to_broadcast([128, 2, 512]), op=ALU.mult),
                        reads=[w_res, invc_r], writes=[w_res])
                    P.op("dve", lambda e, wsum=wsum, gp=gp, zz=zz: e.tensor_tensor(
                        out=pl[:, 2 * gp:2 * gp + 2, :], in0=wsum[:, :, 8:520], in1=zz[:, :, 8:520], op=ALU.subtract),
                        reads=[w_res, zp_r], writes=list(pl_r[2 * gp:2 * gp + 2]))

            def final_norm_tile(gi, t):
                s_, o_, _ = c_geom(gi)
                rsv, rs_res, b = rms_stats(xs[:, t, :], xs_r[t])
                P.op("dve", lambda e, rsv=rsv: e.scalar_tensor_tensor(
                    out=xs[:, t, :], in0=xs[:, t, :], scalar=rsv, in1=gfin[:], op0=ALU.mult, op1=ALU.mult),
                    reads=[xs_r[t], rs_res, gfin_r], writes=[xs_r[t]])
                r0 = s_ * 2048 + o_ + t * 128
                P.op("sp", lambda e, r0=r0: e.dma_start(out=y_out[r0:r0 + 128, :], in_=xs[:, t, :]),
                     reads=[xs_r[t]], dma=True, key=xs_r[t])

            def load_h1_tile(gi, t):
                r0 = c_tok0(gi) + t * 128
                P.op("sp", lambda e, r0=r0: e.dma_start(out=xs[:, t, :], in_=H1[r0:r0 + 128, :]),
                     writes=[xs_r[t]], dma=True, key=xs_r[t])

            mixer_loads_early(0)
            mixer_pre(0)
            for gi in range(8):
                s = gi // 4
                o = (gi % 4) * 512
                own_base = 0 if s == 0 else 2048 + 256
                tok0 = own_base + o
                if gi == 0:
                    load_u(0)
                    for t in range(4):
                        load_h1_tile(0, t)
                def pool_mix(gi=gi):
                    wpl, wpl_r, ipl = ws.get("c%dpool" % gi)
                    for gp in range(4):
                        for oc in range(2):
                            po, po_r = fbank()

                            def mmp(e, po=po, gp=gp, oc=oc, wpl=wpl):
                                for k in range(2):
                                    i = e.matmul(po, lhsT=wpl[:, gp * 2 + k, oc * 128:(oc + 1) * 128], rhs=pl[:, gp * 2 + k, :],
                                                 start=(k == 0), stop=(k == 1))
                                return i
                            P.op("pe", mmp, reads=list(pl_r[2 * gp:2 * gp + 2]) + [wpl_r], writes=[po_r])
                            ci = gp * 2 + oc
                            P.op("act", lambda e, po=po, ci=ci: e.activation(out=pmT[:, ci, :], in_=po, func=AF.Copy,
                                                                             scale=psT[:, ci:ci + 1]),
                                 reads=[po_r, psT_r], writes=[pmT_r[ci]])
                    ws.done(ipl)
                for mb in range(4):
                    for hf in range(2):
                        wga, wga_r, iga = ws.get("c%dga%d_%d" % (gi, mb, hf))
                        wgp, wgp_r, igp = ws.get("c%dgp%d_%d" % (gi, mb, hf))
                        first = (mb == 0 and hf == 0)
                        if not first:
                            (wba_, wbp_), wbb_r, ibb = ws.get("c%dbb%d_%d" % (gi, mb, hf))
                        for j2 in range(2):
                            m = mb * 4 + hf * 2 + j2
                            (p0, p0_r), (p1, p1_r) = fbank(), fbank()

                            def mm16(e, w, o, j2=j2):
                                for k in range(16):
                                    i = e.matmul(o, lhsT=w[:, k, j2 * 128:(j2 + 1) * 128], rhs=uT[:, k, :],
                                                 start=(k == 0), stop=(k == 15))
                                return i

                            def mm8(e, w, o, src, j2=j2):
                                for k in range(8):
                                    i = e.matmul(o, lhsT=w[:, k, j2 * 128:(j2 + 1) * 128], rhs=src[:, k, :],
                                                 start=(k == 0), stop=(k == 7))
                                return i
                            P.op("pe", lambda e, w=wga, o=p0, f=mm16: f(e, w, o), reads=uT_r + [wga_r], writes=[p0_r])
                            P.op("pe", lambda e, w=wgp, o=p1, f=mm16: f(e, w, o), reads=uT_r + [wgp_r], writes=[p1_r])
                            P.op("act", lambda e, p0=p0: e.activation(out=sg[:, 0, :], in_=p0, func=AF.Sigmoid),
                                 reads=[p0_r], writes=[sg_r[0]])
                            P.op("act", lambda e, p1=p1: e.activation(out=sg[:, 1, :], in_=p1, func=AF.Sigmoid),
                                 reads=[p1_r], writes=[sg_r[1]])
                            if first and j2 == 0:
                                pool_mix()
                                (wba_, wbp_), wbb_r, ibb = ws.get("c%dbb%d_%d" % (gi, mb, hf))
                            (p2, p2_r), (p3, p3_r) = fbank(), fbank()
                            P.op("pe", lambda e, w=wba_, o=p2, f=mm8: f(e, w, o, yaT), reads=list(yaT_r) + [wbb_r], writes=[p2_r])
                            P.op("pe", lambda e, w=wbp_, o=p3, f=mm8: f(e, w, o, pmT), reads=list(pmT_r) + [wbb_r], writes=[p3_r])
                            P.op("dve", lambda e, p2=p2: e.tensor_tensor(out=t1[:, 0, :], in0=sg[:, 0, :], in1=p2, op=ALU.mult),
                                 reads=[sg_r[0], p2_r], writes=[t1_r[0]])
                            P.op("dve", lambda e, p3=p3: e.tensor_tensor(out=t1[:, 1, :], in0=sg[:, 1, :], in1=p3, op=ALU.mult),
                                 reads=[sg_r[1], p3_r], writes=[t1_r[1]])
                            P.op("dve", lambda e, m=m: e.tensor_tensor(out=mT[:, m, :], in0=t1[:, 0, :], in1=t1[:, 1, :], op=ALU.add),
                                 reads=t1_r, writes=[mT_r[m]])
                            if gi > 0 and m in (1, 3, 5, 7):
                                final_norm_tile(gi - 1, m // 2)
                                load_h1_tile(gi, m // 2)
                        ws.done(iga)
                        ws.done(igp)
                        ws.done(ibb)
                for c in range(8):
                    wo_, wo_r, iwo = ws.get("c%dwo%d" % (gi, c))
                    for t in range(4):
                        po, po_r = fbank()

                        def mmo(e, po=po, t=t, wo_=wo_):
                            for k in range(16):
                                i = e.matmul(po[:, 0:256], lhsT=mT[:, k, t * 128:(t + 1) * 128], rhs=wo_[:, k, :],
                                             start=(k == 0), stop=(k == 15))
                            return i
                        P.op("pe", mmo, reads=list(mT_r) + [wo_r], writes=[po_r])
                        P.op("dve", lambda e, po=po, t=t, c=c: e.tensor_tensor(
                            out=xs[:, t, c * 256:(c + 1) * 256], in0=po[:, 0:256], in1=xs[:, t, c * 256:(c + 1) * 256], op=ALU.add),
                            reads=[po_r, xs_r[t]], writes=[xs_r[t]])
                    ws.done(iwo)
                norm_group(hs_tiles, 2, uT, uT_r)
                ffn(ws, "c%d" % gi, aT, aT_r, hs_tiles,
                    mid_hook=(lambda gi=gi: (load_u(gi + 1), mixer_loads_early(gi + 1))) if gi + 1 < 8 else None)
                if gi + 1 < 8:
                    mixer_pre(gi + 1)
                if gi == 7:
                    for t in range(4):
                        final_norm_tile(gi, t)
            P.barrier()
            P.emit()
        P.final_wait()
    return nc


def _bias_tables(rpb, interior=False):
    rpb = np.asarray(rpb, np.float32).reshape(16, 15, 31)
    T = np.full((8, 2, 64, 2, 22, 64), NEG, np.float32)
    qc = np.arange(64)
    cs = np.clip(qc - 8, 0, 48)
    kc = np.arange(64)
    colok = (kc[None, :] >= cs[:, None]) & (kc[None, :] < cs[:, None] + 16)
    dc = np.clip(kc[None, :] - qc[:, None] + 15, 0, 30)
    for qrl in range(2):
        for i in range(22):
            dr = i - qrl - 3
            if (3 <= dr <= 10) if interior else (0 <= dr <= 14):
                vals = rpb[:, dr, :][:, dc]
                vals = np.where(colok[None], vals, np.float32(NEG))
                T[:, qrl, :, :, i, :] = vals.reshape(8, 2, 64, 64).transpose(0, 2, 1, 3)
    return np.ascontiguousarray(T.reshape(8, 128, 2 * 22 * 64))


def _rowmask(core):
    rm = np.full((2, 12, 2048), NEG, np.float32)
    for s in range(2):
        L = 32 if s == 0 else 40
        own0 = 0 if s == 0 else 4
        for qt in range(16):
            r = own0 + 2 * qt
            nbr, kb = _band(s, qt)
            for qrl in range(2):
                if s == 0:
                    Rg, rows, koff = r + qrl, 32, 0
                else:
                    Rg, rows, koff = 32 * core - 4 + r + qrl, 256, 32 * core - 4
                rs = min(max(Rg - 4, 0), rows - 8)
                for krl in range(nbr):
                    kg = koff + kb + krl
                    if rs <= kg < rs + 8:
                        rm[s, krl, qt * 128 + qrl * 64: qt * 128 + qrl * 64 + 64] = 0.0
    return rm


def _invcnt(core):
    out = np.zeros((2, 4, 2048), np.float32)
    for s in range(2):
        T = 2048 if s == 0 else 16384
        t = np.arange(2048) + (0 if s == 0 else core * 2048)
        for gi, w in enumerate((2, 4, 8, 16)):
            lo = np.clip(t - w // 2, 0, T)
            hi = np.clip(t + w // 2, 0, T)
            out[s, gi] = 1.0 / (hi - lo).astype(np.float32)
    return out


_NC_CACHE = {}


def _prep(x_prompt, x_sample, g_ffn1, w1_gate, w1_up, w1_down, g_mix, w_in, rpb, w_pool,
          pool_scale, w_branch_attn, w_branch_pool, w_out, g_ffn2, w2_gate, w2_up, w2_down, g_final,
          cores=range(NCORES)):
    f = lambda a: np.ascontiguousarray(np.asarray(a, dtype=np.float32))
    x_prompt, x_sample = f(x_prompt), f(x_sample)
    fm = lambda g: f(g).reshape(-1, 128).T
    gT = np.ascontiguousarray(np.concatenate([fm(g_ffn1), fm(g_mix), fm(g_ffn2)], axis=1))
    psT = np.ascontiguousarray(fm(pool_scale))
    gfin = np.ascontiguousarray(np.broadcast_to(f(g_final).reshape(1, D), (128, D)))
    bias_t = _bias_tables(rpb)
    bias_i = _bias_tables(rpb, interior=True)
    e12 = np.zeros((12, 12, 64), np.float32)
    e12[np.arange(12), np.arange(12), :] = 1.0
    e12 = e12.reshape(12, 768)
    shared = {
        "w1_gate": f(w1_gate).reshape(D, DFF), "w1_up": f(w1_up).reshape(D, DFF), "w1_down": f(w1_down).reshape(DFF, D),
        "w2_gate": f(w2_gate).reshape(D, DFF), "w2_up": f(w2_up).reshape(D, DFF), "w2_down": f(w2_down).reshape(DFF, D),
        "w_in": f(w_in).reshape(D, 8192), "w_pool": f(w_pool).reshape(4, 256, 256),
        "w_branch_attn": f(w_branch_attn).reshape(1024, D), "w_branch_pool": f(w_branch_pool).reshape(1024, D),
        "w_out": f(w_out).reshape(D, D), "gT": gT, "psT": psT, "gfin": gfin, "bias_t": bias_t, "bias_i": bias_i, "e12": e12,
    }
    xs_pad = np.zeros((256 * 64 + 8 * 64, D), np.float32)
    xs_pad[256:256 + 16384] = x_sample[0]
    in_maps = []
    for c in cores:
        x_all = np.concatenate([x_prompt[c], xs_pad[c * 2048: c * 2048 + 2560]], axis=0)
        m = dict(shared)
        m["x_all"] = np.ascontiguousarray(x_all)
        m["rowmask"] = _rowmask(c)
        m["invcnt"] = _invcnt(c)
        in_maps.append(m)
    return in_maps


def kernel(**inputs):
    in_maps = _prep(**inputs)
    if "nc" not in _NC_CACHE:
        _NC_CACHE["nc"] = build_nc()
    res = run_bass_kernel_spmd(_NC_CACHE["nc"], in_maps, core_ids=list(range(NCORES)))
    y_prompt = np.empty((8, 2048, D), np.float32)
    y_sample = np.empty((1, 16384, D), np.float32)
    for c in range(NCORES):
        y = res.results[c]["y"]
        y_prompt[c] = y[:2048]
        y_sample[0, c * 2048:(c + 1) * 2048] = y[2048:]
    return (y_prompt, y_sample)
```

```python
import numpy as np
import concourse.bass as bass
import concourse.mybir as mybir
from concourse.bass_utils import run_bass_kernel_spmd

F32 = mybir.dt.float32
BF16 = mybir.dt.bfloat16
AF = mybir.ActivationFunctionType
ALU = mybir.AluOpType
AX = mybir.AxisListType

D = 2048
DFF = 5632
NJ = DFF // 128
NTOK = 4608
NOWN = 4096
NEG = -30000.0
NCORES = 8


class Res:
    __slots__ = ("name", "last_w", "readers")

    def __init__(self, name):
        self.name = name
        self.last_w = None
        self.readers = []


class Op:
    __slots__ = ("eng", "fn", "deps", "is_dma", "key", "kidx", "needed", "sig")

    def __init__(self, eng, fn, is_dma=False, key=None):
        self.eng = eng
        self.fn = fn
        self.deps = []
        self.is_dma = is_dma
        self.key = key
        self.kidx = 0
        self.needed = False
        self.sig = 0


class Prog:
    ENGS = ("pe", "act", "dve", "pool", "sp")

    def __init__(self, nc, sems):
        self.nc = nc
        self.eobj = dict(pe=nc.tensor, act=nc.scalar, dve=nc.vector, pool=nc.gpsimd, sp=nc.sync)
        self.esem = {e: sems.pop() for e in self.ENGS}
        self.free_sems = sems
        self.key_sem = {}
        self.key_cnt = {}
        self.ecnt = {e: 0 for e in self.ENGS}
        self.known = {e: {} for e in self.ENGS}
        self.ops = []
        self.last_eng_op = {e: None for e in self.ENGS}
        self.last_key_op = {}
        self.pending_barrier = None

    def _add_dep(self, op, d):
        if d is not None and d is not op:
            op.deps.append(d)

    def op(self, eng, fn, reads=(), writes=(), acc=(), dma=False, key=None):
        o = Op(eng, fn, dma, key)
        if self.pending_barrier is not None and eng not in self.pending_barrier[1]:
            for d in self.pending_barrier[0]:
                self._add_dep(o, d)
            self.pending_barrier[1].add(eng)
        for r in reads:
            self._add_dep(o, r.last_w)
        for w in writes:
            self._add_dep(o, w.last_w)
            for rd in w.readers:
                self._add_dep(o, rd)
        for r in reads:
            r.readers.append(o)
        for w in writes:
            w.last_w = o
            w.readers = []
        for w in acc:
            w.last_w = o
        if dma:
            assert key is not None
            if key not in self.key_sem:
                self.key_sem[key] = self.free_sems.pop()
                self.key_cnt[key] = 0
            self.key_cnt[key] += 1
            o.kidx = self.key_cnt[key]
            self.last_key_op[key] = o
        else:
            self.last_eng_op[eng] = o
        self.ops.append(o)
        return o

    def barrier(self):
        deps = [o for o in self.last_eng_op.values() if o is not None]
        deps += list(self.last_key_op.values())
        self.pending_barrier = (deps, set())

    def emit(self):
        for o in self.ops:
            for d in o.deps:
                d.needed = True
        for e in self.ENGS:
            if self.last_eng_op[e] is not None:
                self.last_eng_op[e].needed = True
        for o in self.ops:
            if not o.is_dma and o.needed and o.sig == 0:
                self.ecnt[o.eng] += 1
                o.sig = self.ecnt[o.eng]
        streams = {e: [] for e in self.ENGS}
        for o in self.ops:
            kn = self.known[o.eng]
            waits = []
            for d in o.deps:
                if d.is_dma:
                    s, v = self.key_sem[d.key], 16 * d.kidx
                else:
                    s, v = self.esem[d.eng], d.sig
                if kn.get(s, 0) < v:
                    kn[s] = v
                    waits.append((s, v))
            streams[o.eng].append((waits, o))
        self.ops = []
        with self.nc.Block() as block:
            for e in self.ENGS:
                items = streams[e]
                if not items:
                    continue

                def body(eng, items=items, e=e):
                    for waits, o in items:
                        best = {}
                        for s, v in waits:
                            best[s] = max(best.get(s, 0), v)
                        for s, v in best.items():
                            eng.wait_ge(s, v)
                        if o.fn is None:
                            continue
                        ins = o.fn(eng)
                        if o.is_dma:
                            ins.then_inc(self.key_sem[o.key], 16)
                        elif o.needed:
                            ins.then_inc(self.esem[e], 1)

                getattr(block, {"pe": "tensor", "act": "scalar", "dve": "vector",
                                "pool": "gpsimd", "sp": "sync"}[e])(body)

    def final_wait(self):
        items = []
        for k, s in self.key_sem.items():
            items.append((s, 16 * self.key_cnt[k]))
        with self.nc.Block() as block:
            def body(eng):
                for s, v in items:
                    eng.wait_ge(s, v)
            block.sync(body)


def _band(s, qt):
    L, own0 = (32, 0) if s == 0 else (40, 4)
    r = own0 + 2 * qt
    wide = (r >= 28) if s == 0 else (r in (4, 6, 34))
    nbr = 12 if wide else 9
    return nbr, min(max(r - 4, 0), L - nbr)


def _clipped(s, qt):
    r = (0 if s == 0 else 4) + 2 * qt
    return r in ((0, 2, 28, 30) if s == 0 else (4, 6, 32, 34))


class WStream:
    NSLOT = 6

    def __init__(self, P, ring, ring_r, sched, ngroups=1, cache=None):
        self.P, self.ring, self.ring_r, self.sched = P, ring, ring_r, sched
        self.nload = 0
        self.nget = 0
        self.free = list(range(self.NSLOT))
        self.slot_of = {}
        self.cache = cache
        assert len(sched) % ngroups == 0
        self.per_group = len(sched) // ngroups
        self.cres = [Res("wc%d" % j) for j in range(self.per_group)]

    def _pump(self):
        while self.nload < len(self.sched) and self.free:
            i = self.nload
            tag, src, nk, nc_ = self.sched[i]
            s = self.free.pop(0)
            self.slot_of[i] = s
            parts = src if isinstance(src, list) else [(src, nk, nc_)]
            ne = sum(a * b for (_, a, b) in parts)
            g, j = i // self.per_group, i % self.per_group
            rr = self.ring_r[s]
            wbg = j % 2 if self.per_group * 2 <= len(self.sched) else 0
            if self.cache is not None and g > wbg:
                self.P.op("pool", lambda e, s=s, j=j, ne=ne: e.dma_start(out=self.ring[:, s, 0:ne], in_=self.cache[j, :, 0:ne]),
                          reads=[self.cres[j]], writes=[rr], dma=True, key=rr)
            else:
                off = 0
                for (src_, nk_, ncc_) in parts:
                    dst = self.ring[:, s, off:off + nk_ * ncc_].rearrange("p (k c) -> p k c", k=nk_)
                    off += nk_ * ncc_
                    self.P.op("pool", lambda e, dst=dst, src_=src_: e.dma_start(out=dst, in_=src_),
                              writes=[rr], dma=True, key=rr)
                if self.cache is not None and g == wbg:
                    self.P.op("sp", lambda e, s=s, j=j, ne=ne: e.dma_start(out=self.cache[j, :, 0:ne], in_=self.ring[:, s, 0:ne]),
                              reads=[rr], writes=[self.cres[j]], dma=True, key=rr)
            self.nload += 1

    def get(self, tag):
        self._pump()
        i = self.nget
        t, src, nk, nc_ = self.sched[i]
        assert t == tag, (t, tag)
        assert i in self.slot_of, "weight ring exhausted"
        s = self.slot_of[i]
        self.nget += 1
        if isinstance(src, list):
            view, off = [], 0
            for (src_, nk_, ncc_) in src:
                view.append(self.ring[:, s, off:off + nk_ * ncc_].rearrange("p (k c) -> p k c", k=nk_))
                off += nk_ * ncc_
        else:
            view = self.ring[:, s, 0:nk * nc_].rearrange("p (k c) -> p k c", k=nk)
        return view, self.ring_r[s], i

    def done(self, i):
        self.free.append(self.slot_of.pop(i))
        self._pump()


def build_nc(dbg=False):
    nc = bass.Bass("TRN2", target_bir_lowering=False)
    ein = lambda n, shp: nc.dram_tensor(n, shp, F32, kind="ExternalInput").ap()
    x_all = ein("x_all", [NTOK, D])
    w1g, w1u, w1d = ein("w1_gate", [D, DFF]), ein("w1_up", [D, DFF]), ein("w1_down", [DFF, D])
    w2g, w2u, w2d = ein("w2_gate", [D, DFF]), ein("w2_up", [D, DFF]), ein("w2_down", [DFF, D])
    w_in = ein("w_in", [D, 8192])
    w_pool = ein("w_pool", [4, 256, 256])
    w_ba, w_bp = ein("w_branch_attn", [1024, D]), ein("w_branch_pool", [1024, D])
    w_out = ein("w_out", [D, D])
    gT_d = ein("gT", [128, 48])
    psT_d = ein("psT", [128, 8])
    gfin_d = ein("gfin", [128, D])
    bias_d = ein("bias_t", [8, 128, 2 * 22 * 64])
    biasi_d = ein("bias_i", [8, 128, 2 * 22 * 64])
    rm_d = ein("rowmask", [2, 12, 2048])
    e12_d = ein("e12", [12, 768])
    invc_d = ein("invcnt", [2, 4, 2048])
    y_out = nc.dram_tensor("y", [NOWN, D], F32, kind="ExternalOutput").ap()
    skind = "ExternalOutput" if dbg else "Internal"
    H1 = nc.dram_tensor("H1", [NTOK, D], F32, kind=skind).ap()
    QKV = nc.dram_tensor("QKV", [3072, NTOK], BF16, kind=skind).ap()
    ZP = nc.dram_tensor("ZP", [1024, NTOK], F32, kind=skind).ap()
    YATT = nc.dram_tensor("YATT", [1024, NOWN], BF16, kind=skind).ap()
    U2T = nc.dram_tensor("U2T", [D, NTOK], BF16, kind=skind).ap()
    WCA = nc.dram_tensor("WCA", [84, 128, 4096], BF16).ap()
    WCC = nc.dram_tensor("WCC", [101, 128, 4096], BF16).ap()

    from contextlib import ExitStack
    with ExitStack() as es:
        sems = [es.enter_context(nc.semaphore("s%d" % i)) for i in range(48)]
        P = Prog(nc, sems)
        sb = lambda n, shp, dt: es.enter_context(nc.sbuf_tensor("k_" + n, shp, dt))
        ident = sb("ident", [128, 128], BF16)
        identf = sb("identf", [128, 128], F32)
        gT = sb("gT_sb", [128, 48], F32)
        psT = sb("psT_sb", [128, 8], F32)
        st_ss = sb("st_ss", [128, 8], F32)
        st_rs = sb("st_rs", [128, 8], F32)
        ssq = sb("ssq", [128, 32], F32)
        pT = es.enter_context(nc.psum_tensor("pT", [128, 2, 1024], BF16))
        pF = es.enter_context(nc.psum_tensor("pF", [128, 6, 512], F32))

        R = Res
        ident_r, gT_r, psT_r = R("ident"), R("gT"), R("psT")
        xs_r = [R("xs%d" % i) for i in range(4)]
        uT_r = [R("uT%d" % i) for i in range(4)]
        un_r = [R("un%d" % i) for i in range(4)]
        ring_r = [R("ring%d" % i) for i in range(6)]
        sg_r = [R("sg%d" % i) for i in range(2)]
        ss_r = [R("ss%d" % i) for i in range(8)]
        ss4_r = [R("ss4_%d" % i) for i in range(2)]
        ssq_r = R("ssq")
        rs_r = [R("rs%d" % i) for i in range(8)]
        pT_r = [R("pT%d" % i) for i in range(2)]
        pF_r = [R("pF%d" % i) for i in range(6)]
        cnt = {"pT": 0, "pF": 0, "un": 0, "st": 0, "sg": 0, "ev": 0, "st4": 0}

        def nxt(k, n):
            v = cnt[k] % n
            cnt[k] += 1
            return v

        def fbank():
            b = nxt("pF", 6)
            return pF[:, b, :], pF_r[b]

        P.op("pool", lambda e: e.memset(identf[:], 0.0), writes=[ident_r])
        P.op("pool", lambda e: e.affine_select(out=identf[:], in_=identf[:], pattern=[[-1, 128]],
                                               compare_op=ALU.not_equal, fill=1.0, base=0,
                                               channel_multiplier=1), writes=[ident_r])
        P.op("dve", lambda e: e.tensor_copy(out=ident[:], in_=identf[:]), writes=[ident_r])
        P.op("sp", lambda e: e.dma_start(out=gT[:], in_=gT_d[:, :]), writes=[gT_r], dma=True, key=gT_r)
        P.op("sp", lambda e: e.dma_start(out=psT[:], in_=psT_d[:, :]), writes=[psT_r], dma=True, key=psT_r)

        def rms_stats(src, src_r):
            b = nxt("un", 4)
            s = nxt("st", 8)
            ssv, rsv = st_ss[:, s:s + 1], st_rs[:, s:s + 1]
            P.op("dve", lambda e: e.memset(ssv, 0.0), writes=[ss_r[s]])
            P.op("act", lambda e: e.activation(out=un[:, b, :], in_=src, func=AF.Square,
                                               scale=float(D ** -0.5), accum_out=ssv),
                 reads=[src_r], writes=[un_r[b], ss_r[s]])
            P.op("dve", lambda e: e.tensor_scalar_add(out=ssv, in0=ssv, scalar1=1e-6),
                 reads=[ss_r[s]], writes=[ss_r[s]])
            P.op("act", lambda e: e.activation(out=rsv, in_=ssv, func=AF.Sqrt),
                 reads=[ss_r[s]], writes=[rs_r[s]])
            P.op("dve", lambda e: e.reciprocal(out=rsv, in_=rsv), reads=[rs_r[s]], writes=[rs_r[s]])
            return rsv, rs_r[s], b

        def norm_transpose(src, src_r, gidx, t):
            rsv, rs_res, b = rms_stats(src, src_r)
            P.op("act", lambda e: e.activation(out=un[:, b, :], in_=src, func=AF.Copy, scale=rsv),
                 reads=[src_r, rs_res], writes=[un_r[b]])
            for h in range(2):
                pb = nxt("pT", 2)

                def tr(e, h=h, pb=pb):
                    for kk in range(8):
                        k = h * 8 + kk
                        i = e.transpose(out=pT[:, pb, kk * 128:(kk + 1) * 128],
                                        in_=un[:, b, k * 128:(k + 1) * 128], identity=ident[:])
                    return i
                P.op("pe", tr, reads=[un_r[b], ident_r], writes=[pT_r[pb]])
                gsl = gT[:, gidx * 16 + h * 8: gidx * 16 + h * 8 + 8]
                P.op("dve", lambda e, h=h, pb=pb, gsl=gsl: e.tensor_tensor(
                    out=uT[:, h * 8:(h + 1) * 8, t * 128:(t + 1) * 128],
                    in0=pT[:, pb, :].rearrange("p (k c) -> p k c", k=8),
                    in1=gsl.unsqueeze(2).to_broadcast([128, 8, 128]), op=ALU.mult),
                    reads=[pT_r[pb], gT_r], writes=[uT_r[t]])

        def norm_scale(tiles, pre=None):
            g4 = nxt("st4", 2)
            ss4, rs4 = st_ss[:, g4 * 4:g4 * 4 + 4], st_rs[:, g4 * 4:g4 * 4 + 4]
            res4 = ss4_r[g4]
            if pre is not None:
                P.op("dve", lambda e: e.tensor_reduce(out=ss4, in_=ssq[:, 0:4 * pre].rearrange("p (t c) -> p t c", c=pre),
                                                      axis=AX.X, op=ALU.add), reads=[ssq_r], writes=[res4])
            else:
                P.op("dve", lambda e: e.memset(ss4, 0.0), writes=[res4])
                for t, (src, src_r) in enumerate(tiles):
                    P.op("act", lambda e, t=t, src=src: e.activation(out=un[:, t, :], in_=src, func=AF.Square,
                                                                     scale=float(D ** -0.5), accum_out=ss4[:, t:t + 1]),
                         reads=[src_r, res4], writes=[un_r[t], res4])
            P.op("dve", lambda e: e.tensor_scalar_add(out=ss4, in0=ss4, scalar1=1e-6), reads=[res4], writes=[res4])
            P.op("act", lambda e: e.activation(out=rs4, in_=ss4, func=AF.Sqrt), reads=[res4], writes=[res4])
            P.op("dve", lambda e: e.reciprocal(out=rs4, in_=rs4), reads=[res4], writes=[res4])
            for t, (src, src_r) in enumerate(tiles):
                if t % 2 == 0:
                    P.op("act", lambda e, t=t, src=src: e.activation(out=un[:, t, :], in_=src, func=AF.Copy,
                                                                     scale=rs4[:, t:t + 1]),
                         reads=[src_r, res4], writes=[un_r[t]])
                else:
                    P.op("dve", lambda e, t=t, src=src: e.tensor_scalar(out=un[:, t, :], in0=src, scalar1=rs4[:, t:t + 1],
                                                                        scalar2=None, op0=ALU.mult),
                         reads=[src_r, res4], writes=[un_r[t]])

        def norm_tr(gidx, uTd, uTd_r):
            for t in range(4):
                for h in range(2):
                    pb = nxt("pT", 2)

                    def tr(e, h=h, pb=pb, t=t):
                        for kk in range(8):
                            k = h * 8 + kk
                            i = e.transpose(out=pT[:, pb, kk * 128:(kk + 1) * 128],
                                            in_=un[:, t, k * 128:(k + 1) * 128], identity=ident[:])
                        return i
                    P.op("pe", tr, reads=[un_r[t], ident_r], writes=[pT_r[pb]])
                    gsl = gT[:, gidx * 16 + h * 8: gidx * 16 + h * 8 + 8]
                    P.op("dve", lambda e, h=h, pb=pb, gsl=gsl, t=t: e.tensor_tensor(
                        out=uTd[:, h * 8:(h + 1) * 8, t * 128:(t + 1) * 128],
                        in0=pT[:, pb, :].rearrange("p (k c) -> p k c", k=8),
                        in1=gsl.unsqueeze(2).to_broadcast([128, 8, 128]), op=ALU.mult),
                        reads=[pT_r[pb], gT_r], writes=[uTd_r[t]])

        def norm_group(tiles, gidx, uTd, uTd_r, pre=None):
            norm_scale(tiles, pre)
            norm_tr(gidx, uTd, uTd_r)

        def partial_sq(hv, hr, c0, c1, col):
            P.op("act", lambda e: e.activation(out=sg[:, 0, 0:c1 - c0], in_=hv[:, c0:c1], func=AF.Square,
                                               scale=float(D ** -0.5), accum_out=ssq[:, col:col + 1]),
                 reads=[hr, ssq_r], writes=[sg_r[0], ssq_r])

        DQ = [(0, 8), (8, 8), (16, 8), (24, 8), (32, 8), (40, 4)]

        def ffn_sched(wg, wu, wd, name):
            s = []
            wgv = wg.rearrange("(k p) c -> p k c", p=128)
            wuv = wu.rearrange("(k p) c -> p k c", p=128)
            wdv = wd.rearrange("(j p) c -> p j c", p=128)
            for jb in range(22):
                s.append((name + "g%d" % jb, wgv[:, :, jb * 256:(jb + 1) * 256], 16, 256))
                s.append((name + "u%d" % jb, wuv[:, :, jb * 256:(jb + 1) * 256], 16, 256))
            for c in range(4):
                for qi, (q0, qn) in enumerate(DQ):
                    s.append((name + "d%d_%d" % (c, qi), wdv[:, q0:q0 + qn, c * 512:(c + 1) * 512], qn, 512))
            return s

        def ffn(ws, name, aT, aT_r, hs_tiles, uT=None, uT_r=None, mid_hook=None, stats=False):
            uT, uT_r = (uT_main, uT_main_r) if uT is None else (uT, uT_r)
            for jb in range(22):
                wgs, wg_r, ig = ws.get(name + "g%d" % jb)
                wus, wu_r, iu = ws.get(name + "u%d" % jb)
                for j in range(2):
                    jj = jb * 2 + j
                    pg, pg_r = fbank()
                    pu, pu_r = fbank()

                    def mmg(e, w=wgs, o=pg, j=j):
                        for k in range(16):
                            i = e.matmul(o, lhsT=w[:, k, j * 128:(j + 1) * 128], rhs=uT[:, k, :],
                                         start=(k == 0), stop=(k == 15))
                        return i
                    P.op("pe", mmg, reads=uT_r + [wg_r], writes=[pg_r])
                    P.op("pe", lambda e, w=wus, o=pu, j=j, f=mmg: f(e, w, o, j), reads=uT_r + [wu_r], writes=[pu_r])
                    sb_ = nxt("sg", 2)
                    P.op("act", lambda e, sb_=sb_, pg=pg: e.activation(out=sg[:, sb_, :], in_=pg, func=AF.Silu),
                         reads=[pg_r], writes=[sg_r[sb_]])
                    P.op("dve", lambda e, sb_=sb_, pu=pu, jj=jj: e.tensor_tensor(
                        out=aT[:, jj, :], in0=sg[:, sb_, :], in1=pu, op=ALU.mult),
                        reads=[sg_r[sb_], pu_r], writes=[aT_r[jj]])
                ws.done(ig)
                ws.done(iu)
            if mid_hook is not None:
                mid_hook()
            if stats:
                P.op("dve", lambda e: e.memset(ssq[:, 0:16], 0.0), writes=[ssq_r])
            for c in range(4):
                banks = [fbank() for _ in range(4)]
                for qi, (q0, qn) in enumerate(DQ):
                    wds, wd_r, idd = ws.get(name + "d%d_%d" % (c, qi))
                    for t in range(4):
                        pd, pd_r = banks[t]

                        def mmd(e, w=wds, o=pd, q0=q0, qn=qn, t=t):
                            for j in range(qn):
                                i = e.matmul(o, lhsT=aT[:, q0 + j, t * 128:(t + 1) * 128], rhs=w[:, j, :],
                                             start=(q0 + j == 0), stop=(q0 + j == NJ - 1))
                            return i
                        rd = aT_r[q0:q0 + qn] + [wd_r]
                        if qi == 0:
                            P.op("pe", mmd, reads=rd, writes=[pd_r])
                        else:
                            P.op("pe", mmd, reads=rd, acc=[pd_r])
                    ws.done(idd)
                for t in range(4):
                    pd, pd_r = banks[t]
                    hv, hr = hs_tiles[t]
                    P.op("dve", lambda e, pd=pd, hv=hv, c=c: e.scalar_tensor_tensor(
                        out=hv[:, c * 512:(c + 1) * 512], in0=pd, scalar=0.5, in1=hv[:, c * 512:(c + 1) * 512],
                        op0=ALU.mult, op1=ALU.add), reads=[pd_r, hr], writes=[hr])
                    if stats:
                        partial_sq(hv, hr, c * 512, (c + 1) * 512, t * 4 + c)

        w_in_v = w_in.rearrange("(k p) c -> p k c", p=128)

        with ExitStack() as pa:
            xs = pa.enter_context(nc.sbuf_tensor("a_xs", [128, 4, D], F32))
            uT = pa.enter_context(nc.sbuf_tensor("a_uT", [128, 16, 512], BF16))
            un = pa.enter_context(nc.sbuf_tensor("a_un", [128, 4, D], BF16))
            ring = pa.enter_context(nc.sbuf_tensor("a_ring", [128, 6, 4096], BF16))
            sg = pa.enter_context(nc.sbuf_tensor("a_sg", [128, 2, 512], F32))
            hs_tiles = [(xs[:, t, :], xs_r[t]) for t in range(4)]
            uT_main, uT_main_r = uT, uT_r
            aT = pa.enter_context(nc.sbuf_tensor("aT_a", [128, NJ, 512], BF16))
            stgb = pa.enter_context(nc.sbuf_tensor("stgb", [128, 2, 4, 512], BF16))
            stgf = pa.enter_context(nc.sbuf_tensor("stgf", [128, 2, 4, 512], F32))
            aT_r = [R("aT%d" % j) for j in range(NJ)]
            stgb_r = [R("stgb%d" % i) for i in range(2)]
            stgf_r = [R("stgf%d" % i) for i in range(2)]
            NGA = NTOK // 512
            sched = []
            for gi in range(NGA):
                sched += ffn_sched(w1g, w1u, w1d, "a%d" % gi)
                for nb in range(16):
                    sched.append(("a%din%d" % (gi, nb), w_in_v[:, :, nb * 256:(nb + 1) * 256], 16, 256))
            ws = WStream(P, ring, ring_r, sched, ngroups=NGA, cache=WCA)
            ws._pump()
            uT2 = pa.enter_context(nc.sbuf_tensor("k_uT2", [128, 16, 512], BF16))
            uT2_r = [R("uT2_%d" % i) for i in range(4)]

            def load_x(gi):
                for t in range(4):
                    r0 = gi * 512 + t * 128
                    P.op("sp", lambda e, t=t, r0=r0: e.dma_start(out=xs[:, t, :], in_=x_all[r0:r0 + 128, :]),
                         writes=[xs_r[t]], dma=True, key=xs_r[t])
            load_x(0)
            norm_group(hs_tiles, 0, uT, uT_r)
            for gi in range(NGA):
                tok0 = gi * 512
                ffn(ws, "a%d" % gi, aT, aT_r, hs_tiles, stats=True)
                for t in range(4):
                    r0 = tok0 + t * 128
                    P.op("sp", lambda e, t=t, r0=r0: e.dma_start(out=H1[r0:r0 + 128, :], in_=xs[:, t, :]),
                         reads=[xs_r[t]], dma=True, key=xs_r[t])
                norm_group(hs_tiles, 1, uT2, uT2_r, pre=4)
                P.op("sp", lambda e, tok0=tok0: e.dma_start(
                    out=U2T[:, tok0:tok0 + 512].rearrange("(k p) t -> p k t", p=128), in_=uT2[:]),
                    reads=uT2_r, dma=True, key=uT2_r[0])
                if gi + 1 < NGA:
                    load_x(gi + 1)
                for nb in range(8):
                    sp_ = nb % 2
                    isz = nb >= 6
                    stg, stg_res = (stgf, stgf_r[sp_]) if isz else (stgb, stgb_r[sp_])
                    for half in range(2):
                        wv, w_r, iw = ws.get("a%din%d" % (gi, nb * 2 + half))
                        for j2 in range(2):
                            j = half * 2 + j2
                            po, po_r = fbank()

                            def mmi(e, w=wv, o=po, j2=j2):
                                for k in range(16):
                                    i = e.matmul(o, lhsT=w[:, k, j2 * 128:(j2 + 1) * 128], rhs=uT2[:, k, :],
                                                 start=(k == 0), stop=(k == 15))
                                return i
                            P.op("pe", mmi, reads=uT2_r + [w_r], writes=[po_r])
                            dst = stg[:, sp_, j, :]
                            if nb < 2:
                                P.op("act", lambda e, dst=dst, po=po: e.activation(out=dst, in_=po, func=AF.Copy, scale=0.125),
                                     reads=[po_r], writes=[stg_res])
                            elif nxt("ev", 2) == 0:
                                P.op("act", lambda e, dst=dst, po=po: e.activation(out=dst, in_=po, func=AF.Copy),
                                     reads=[po_r], writes=[stg_res])
                            else:
                                P.op("dve", lambda e, dst=dst, po=po: e.tensor_copy(out=dst, in_=po),
                                     reads=[po_r], writes=[stg_res])
                        ws.done(iw)
                    if isz:
                        dstd = ZP[(nb - 6) * 512:(nb - 5) * 512, tok0:tok0 + 512].rearrange("(j p) t -> p j t", p=128)
                    else:
                        dstd = QKV[nb * 512:(nb + 1) * 512, tok0:tok0 + 512].rearrange("(j p) t -> p j t", p=128)
                    P.op("sp", lambda e, dstd=dstd, stg=stg, sp_=sp_: e.dma_start(out=dstd, in_=stg[:, sp_]),
                         reads=[stg_res], dma=True, key=stg_res)
                    if nb == 1 and gi + 1 < NGA:
                        norm_scale(hs_tiles)
                    if nb == 4 and gi + 1 < NGA:
                        norm_tr(0, uT, uT_r)
            P.barrier()
            P.emit()

        with ExitStack() as pb_:
            sbb = lambda n, shp, dt: pb_.enter_context(nc.sbuf_tensor("b_" + n, shp, dt))
            NSB = 3
            qz = sbb("qz", [128, 2, 2, 2048], BF16)
            kT = sbb("kT", [128, 2, 2560], BF16)
            vT = sbb("vT", [128, 2, 2560], BF16)
            Vp = sbb("Vp", [128, 2, 20, 2, 128], BF16)
            T2 = sbb("T2", [128, 2, 2, 22, 64], F32)
            T2i = sbb("T2i", [128, 2, 2, 22, 64], F32)
            rmf = sbb("rmf", [12, 2048], F32)
            rmT = sbb("rmT", [12, 2, 2048], BF16)
            e12 = sbb("e12", [12, 768], BF16)
            Sb = sbb("Sb", [128, NSB, 768], F32)
            Pf = sbb("Pf", [128, NSB, 768], F32)
            Pn = sbb("Pn", [128, NSB, 768], BF16)
            PT = sbb("PT", [128, 4, 6, 128], BF16)
            yT = sbb("yT", [128, 2, 2048], BF16)
            mx = sbb("mx", [128, 8], F32)
            sm = sbb("sm", [128, 8], F32)
            qT_r = [R("qT%d" % i) for i in range(2)]
            kT_r = [R("kT%d" % i) for i in range(2)]
            vT_r = [R("vT%d" % i) for i in range(2)]
            Vp_r = [R("Vp%d" % i) for i in range(2)]
            T2_r = [R("T2%d" % i) for i in range(2)]
            rm_r, rmf_r, e12_r = R("rm"), R("rmf"), R("e12")
            Sb_r = [R("Sb%d" % i) for i in range(NSB)]
            Pf_r = [R("Pf%d" % i) for i in range(NSB)]
            Pn_r = [R("Pn%d" % i) for i in range(NSB)]
            PT_r = [R("PT%d" % i) for i in range(4)]
            yT_r = [R("yT%d" % i) for i in range(2)]
            mx_r = [R("mx%d" % i) for i in range(8)]
            sm_r = [R("sm%d" % i) for i in range(8)]
            for s in range(2):
                P.op("sp", lambda e, s=s: e.dma_start(out=rmf[:], in_=rm_d[s]), writes=[rmf_r], dma=True, key=rmf_r)
                P.op("dve", lambda e, s=s: e.tensor_copy(out=rmT[:, s, :], in_=rmf[:]), reads=[rmf_r], writes=[rm_r])
            P.op("sp", lambda e: e.dma_start(out=rmf[:, 0:768], in_=e12_d[:, :]), writes=[rmf_r], dma=True, key=rmf_r)
            P.op("dve", lambda e: e.tensor_copy(out=e12[:], in_=rmf[:, 0:768]), reads=[rmf_r], writes=[e12_r])
            P.op("pool", lambda e: e.memset(Vp[:], 0.0), writes=Vp_r)
            P.op("pool", lambda e: e.memset(qz[:], 0.0), writes=qT_r)

            SEQ = [(32, 0, 0), (40, 2048, 4)]
            NBLK = 16
            NIT = 32

            def b_load(bi):
                s, hp, hb = bi // 8, bi % 8, bi % 2
                L, tokb, own0 = SEQ[s]
                q0 = tokb + own0 * 64
                for eh in range(2):
                    P.op("sp", lambda e, eh=eh: e.dma_start(
                        out=qz[eh * 64:(eh + 1) * 64, hb, eh, :],
                        in_=QKV[hp * 128 + eh * 64:hp * 128 + (eh + 1) * 64, q0:q0 + 2048]),
                        writes=[qT_r[hb]], dma=True, key=qT_r[hb])
                P.op("sp", lambda e: e.dma_start(
                    out=kT[:, hb, 0:L * 64], in_=QKV[1024 + hp * 128:1024 + (hp + 1) * 128, tokb:tokb + L * 64]),
                    writes=[kT_r[hb]], dma=True, key=kT_r[hb])
                P.op("sp", lambda e: e.dma_start(
                    out=vT[:, hb, 0:L * 64], in_=QKV[2048 + hp * 128:2048 + (hp + 1) * 128, tokb:tokb + L * 64]),
                    writes=[vT_r[hb]], dma=True, key=vT_r[hb])
                P.op("sp", lambda e: e.dma_start(out=T2[:, hb], in_=bias_d[hp].rearrange("p (e i c) -> p e i c", e=2, i=22)),
                     writes=[T2_r[hb]], dma=True, key=T2_r[hb])
                P.op("sp", lambda e: e.dma_start(out=T2i[:, hb], in_=biasi_d[hp].rearrange("p (e i c) -> p e i c", e=2, i=22)),
                     writes=[T2_r[hb]], dma=True, key=T2_r[hb])

            def b_vtrans(bi, pb):
                s, hb = bi // 8, bi % 2
                nkt = SEQ[s][0] // 2
                for k0 in range(0, nkt, 8):
                    n = min(8, nkt - k0)

                    def trv(e, k0=k0, n=n, pb=pb):
                        for kk in range(n):
                            i = e.transpose(out=pT[:, pb, kk * 128:(kk + 1) * 128],
                                            in_=vT[:, hb, (k0 + kk) * 128:(k0 + kk + 1) * 128], identity=ident[:])
                        return i
                    P.op("pe", trv, reads=[vT_r[hb], ident_r], writes=[pT_r[pb]])
                    pv = pT[:, pb, 0:n * 128].rearrange("p (k c) -> p k c", k=n)
                    P.op("dve", lambda e, k0=k0, n=n, pv=pv: e.tensor_copy(out=Vp[:, hb, k0:k0 + n, 0, 0:64], in_=pv[:, :, 0:64]),
                         reads=[pT_r[pb]], writes=[Vp_r[hb]])
                    P.op("act", lambda e, k0=k0, n=n, pv=pv: e.activation(out=Vp[:, hb, k0:k0 + n, 1, 64:128], in_=pv[:, :, 64:128], func=AF.Copy),
                         reads=[pT_r[pb]], writes=[Vp_r[hb]])

            def geom(n):
                bi, loc = n // NIT, n % NIT
                s, hb = bi // 8, bi % 2
                L, tokb, own0 = SEQ[s]
                qt, eh = loc // 2, loc % 2
                r = own0 + 2 * qt
                nbr, kb = _band(s, qt)
                return dict(s=s, hb=hb, qt=qt, eh=eh, kb=kb, nbr=nbr, i0=kb - r + 10, kt0=kb // 2, clip=_clipped(s, qt),
                            ncol=nbr * 64, nT=(nbr + 1) // 2)

            def stage_S(n):
                g = geom(n)
                s, hb, qt, kb, nbr = g["s"], g["hb"], g["qt"], g["kb"], g["nbr"]
                lo, hi = g["eh"] * 64, g["eh"] * 64 + 64
                ba, bb = (n % 2) * 2, (n % 2) * 2 + 1
                pa_, pbk = pF[:, ba, :], pF[:, bb, :]
                nB = g["ncol"] - 512

                eh, clip = g["eh"], g["clip"]

                def mms(e):
                    ql = qz[:, hb, eh, qt * 128:(qt + 1) * 128]
                    e.matmul(pa_, lhsT=ql, rhs=kT[:, hb, kb * 64:kb * 64 + 512], start=True, stop=not clip)
                    if clip:
                        e.matmul(pa_, lhsT=rmT[0:nbr, s, qt * 128:(qt + 1) * 128], rhs=e12[0:nbr, 0:512],
                                 start=False, stop=True)
                    i = e.matmul(pbk[:, 0:nB], lhsT=ql, rhs=kT[:, hb, kb * 64 + 512:kb * 64 + 512 + nB],
                                 start=True, stop=not clip)
                    if clip:
                        i = e.matmul(pbk[:, 0:nB], lhsT=rmT[0:nbr, s, qt * 128:(qt + 1) * 128], rhs=e12[0:nbr, 512:512 + nB],
                                     start=False, stop=True)
                    return i
                P.op("pe", mms, reads=[qT_r[hb], kT_r[hb], rm_r, e12_r], writes=[pF_r[ba], pF_r[bb]])

            def stage_bias(n):
                g = geom(n)
                hb, eh, i0, nbr, ncol = g["hb"], g["eh"], g["i0"], g["nbr"], g["ncol"]
                ba, bb = (n % 2) * 2, (n % 2) * 2 + 1
                pa_, pbk = pF[:, ba, :], pF[:, bb, :]
                sbi, si = n % NSB, n % 8
                mxv, smv = mx[:, si:si + 1], sm[:, si:si + 1]
                TT = T2 if g["clip"] else T2i
                P.op("dve", lambda e: e.tensor_tensor(
                    out=Sb[:, sbi, 0:512], in0=pa_, in1=TT[:, hb, eh, i0:i0 + 8, :].rearrange("p i c -> p (i c)"),
                    op=ALU.add), reads=[pF_r[ba], T2_r[hb]], writes=[Sb_r[sbi]])
                P.op("dve", lambda e: e.tensor_tensor(
                    out=Sb[:, sbi, 512:ncol], in0=pbk[:, 0:ncol - 512],
                    in1=TT[:, hb, eh, i0 + 8:i0 + nbr, :].rearrange("p i c -> p (i c)"),
                    op=ALU.add), reads=[pF_r[bb], T2_r[hb], Sb_r[sbi]], writes=[Sb_r[sbi]])
                P.op("dve", lambda e: e.reduce_max(out=mxv, in_=Sb[:, sbi, 0:ncol], axis=AX.X),
                     reads=[Sb_r[sbi]], writes=[mx_r[si]])
                P.op("dve", lambda e: e.tensor_scalar_mul(out=mxv, in0=mxv, scalar1=-1.0),
                     reads=[mx_r[si]], writes=[mx_r[si]])
                P.op("dve", lambda e: e.memset(smv, 0.0), writes=[sm_r[si]])

            def stage_exp(n):
                ncol = geom(n)["ncol"]
                sbi, si = n % NSB, n % 8
                mxv, smv = mx[:, si:si + 1], sm[:, si:si + 1]
                P.op("act", lambda e: e.activation(
                    out=Pf[:, sbi, 0:ncol], in_=Sb[:, sbi, 0:ncol], func=AF.Exp, bias=mxv, scale=1.0, accum_out=smv),
                    reads=[Sb_r[sbi], mx_r[si]], writes=[Pf_r[sbi], sm_r[si]])

            def stage_recip(n):
                si = n % 8
                smv = sm[:, si:si + 1]
                P.op("dve", lambda e: e.reciprocal(out=smv, in_=smv), reads=[sm_r[si]], writes=[sm_r[si]])

            def stage_norm(n):
                ncol = geom(n)["ncol"]
                sbi, si = n % NSB, n % 8
                smv = sm[:, si:si + 1]
                P.op("act", lambda e: e.activation(
                    out=Pn[:, sbi, 0:ncol], in_=Pf[:, sbi, 0:ncol], func=AF.Copy, scale=smv),
                    reads=[Pf_r[sbi], sm_r[si]], writes=[Pn_r[sbi]])

            def stage_T(n):
                g = geom(n)
                nT, ncol = g["nT"], g["ncol"]
                sbi, pb = n % NSB, n % 2

                def trp(e):
                    for kk in range(nT):
                        w = min(128, ncol - kk * 128)
                        i = e.transpose(out=pT[0:w, pb, kk * 128:(kk + 1) * 128],
                                        in_=Pn[:, sbi, kk * 128:kk * 128 + w], identity=ident[:])
                    return i
                P.op("pe", trp, reads=[Pn_r[sbi], ident_r], writes=[pT_r[pb]])

            def stage_copy(n):
                nT = geom(n)["nT"]
                pb, pti = n % 2, n % 4
                P.op("act", lambda e: e.activation(
                    out=PT[:, pti, 0:nT, :], in_=pT[:, pb, 0:nT * 128].rearrange("p (k c) -> p k c", k=nT), func=AF.Copy),
                    reads=[pT_r[pb]], writes=[PT_r[pti]])

            def stage_PV(n):
                g = geom(n)
                hb, qt, kt0, nT, ncol = g["hb"], g["qt"], g["kt0"], g["nT"], g["ncol"]
                yb_ = 4 + (qt % 2)
                py = pF[:, yb_, :]

                def mmy(e):
                    for eh_ in range(2):
                        pti = (n - 1 + eh_) % 4
                        for kk in range(nT):
                            w = min(128, ncol - kk * 128)
                            i = e.matmul(py[:, 0:128], lhsT=Vp[0:w, hb, kt0 + kk, eh_, :], rhs=PT[0:w, pti, kk, :],
                                         start=(eh_ == 0 and kk == 0), stop=(eh_ == 1 and kk == nT - 1))
                    return i
                P.op("pe", mmy, reads=[Vp_r[hb], PT_r[(n - 1) % 4], PT_r[n % 4]], writes=[pF_r[yb_]])

            def stage_y(n):
                g = geom(n)
                hb, qt = g["hb"], g["qt"]
                yb_ = 4 + (qt % 2)
                py = pF[:, yb_, :]
                P.op("dve", lambda e: e.tensor_copy(out=yT[:, hb, qt * 128:(qt + 1) * 128], in_=py[:, 0:128]),
                     reads=[pF_r[yb_]], writes=[yT_r[hb]])
                if qt == 15:
                    bi = n // NIT
                    s, hp = bi // 8, bi % 8
                    P.op("sp", lambda e: e.dma_start(
                        out=YATT[hp * 128:(hp + 1) * 128, s * 2048:(s + 1) * 2048], in_=yT[:, hb, :]),
                        reads=[yT_r[hb]], dma=True, key=yT_r[hb])

            b_load(0)
            b_vtrans(0, 0)
            NTOT = NBLK * NIT
            for step in range(NTOT + 8):
                if step < NTOT and step % NIT == 8 and step // NIT + 1 < NBLK:
                    b_load(step // NIT + 1)
                if step < NTOT:
                    stage_S(step)
                if 0 <= step - 3 < NTOT:
                    stage_recip(step - 3)
                if 0 <= step - 1 < NTOT:
                    stage_bias(step - 1)
                if 0 <= step - 2 < NTOT:
                    stage_exp(step - 2)
                if 0 <= step - 3 < NTOT:
                    stage_norm(step - 3)
                if 0 <= step - 4 < NTOT:
                    stage_T(step - 4)
                if 0 <= step - 5 < NTOT:
                    stage_copy(step - 5)
                if 0 <= step - 6 < NTOT and (step - 6) % 2 == 1:
                    stage_PV(step - 6)
                if 0 <= step - 7 < NTOT and (step - 7) % 2 == 1:
                    stage_y(step - 7)
                if step < NTOT and step % NIT == 16 and step // NIT + 1 < NBLK:
                    b_vtrans(step // NIT + 1, (step - 5) % 2)
            P.barrier()
            P.emit()

        with ExitStack() as pc:
            xs = pc.enter_context(nc.sbuf_tensor("c_xs", [128, 4, D], F32))
            uT = pc.enter_context(nc.sbuf_tensor("c_uT", [128, 16, 512], BF16))
            un = pc.enter_context(nc.sbuf_tensor("c_un", [128, 4, D], BF16))
            ring = pc.enter_context(nc.sbuf_tensor("c_ring", [128, 6, 4096], BF16))
            sg = pc.enter_context(nc.sbuf_tensor("c_sg", [128, 2, 512], F32))
            hs_tiles = [(xs[:, t, :], xs_r[t]) for t in range(4)]
            uT_main, uT_main_r = uT, uT_r
            sbc = lambda n, shp, dt: pc.enter_context(nc.sbuf_tensor("c_" + n, shp, dt))
            aT = sbc("aT_c", [128, NJ, 512], BF16)
            aT_r = [R("aTc%d" % j) for j in range(NJ)]
            zp = sbc("zp", [128, 8, 528], F32)
            tA = sbc("tA", [128, 2, 528], F32)
            tB = sbc("tB", [128, 2, 528], F32)
            invc = sbc("invc", [128, 4, 512], F32)
            gfin = sbc("gfin", [128, D], F32)
            t1 = sbc("t1", [128, 2, 512], F32)
            zp_r, tA_r, tB_r, invc_r, gfin_r = R("zp"), R("tA"), R("tB"), R("invc"), R("gfin")
            t1_r = [R("t1_%d" % i) for i in range(2)]
            mT, mT_r = aT[:, 0:16], aT_r[0:16]
            yaT, yaT_r = aT[:, 16:24], aT_r[16:24]
            pmT, pmT_r = aT[:, 24:32], aT_r[24:32]
            pl, pl_r = aT[:, 32:40], aT_r[32:40]
            P.op("sp", lambda e: e.dma_start(out=gfin[:], in_=gfin_d[:, :]), writes=[gfin_r], dma=True, key=gfin_r)
            wpv = w_pool.rearrange("g (k p) d -> p (g k) d", p=128)
            wbav = w_ba.rearrange("(k p) c -> p k c", p=128)
            wbpv = w_bp.rearrange("(k p) c -> p k c", p=128)
            wov = w_out.rearrange("(k p) c -> p k c", p=128)
            sched = []
            for gi in range(8):
                for mb in range(4):
                    for hf in range(2):
                        c0 = mb * 512 + hf * 256
                        sched.append(("c%dga%d_%d" % (gi, mb, hf), w_in_v[:, :, 4096 + c0:4096 + c0 + 256], 16, 256))
                        sched.append(("c%dgp%d_%d" % (gi, mb, hf), w_in_v[:, :, 6144 + c0:6144 + c0 + 256], 16, 256))
                        if mb == 0 and hf == 0:
                            sched.append(("c%dpool" % gi, wpv, 8, 256))
                        sched.append(("c%dbb%d_%d" % (gi, mb, hf),
                                      [(wbav[:, :, c0:c0 + 256], 8, 256), (wbpv[:, :, c0:c0 + 256], 8, 256)], None, None))
                for c in range(8):
                    sched.append(("c%dwo%d" % (gi, c), wov[:, :, c * 256:(c + 1) * 256], 16, 256))
                sched += ffn_sched(w2g, w2u, w2d, "c%d" % gi)
            ws = WStream(P, ring, ring_r, sched, ngroups=8, cache=WCC)
            ws._pump()

            def c_tok0(gi):
                return (0 if gi < 4 else 2048 + 256) + (gi % 4) * 512

            def load_u(gi):
                t0 = c_tok0(gi)
                P.op("sp", lambda e: e.dma_start(out=uT[:], in_=U2T[:, t0:t0 + 512].rearrange("(k p) t -> p k t", p=128)),
                     writes=uT_r, dma=True, key=uT_r[0])
            def c_geom(gi):
                return gi // 4, (gi % 4) * 512, c_tok0(gi)

            def mixer_loads_early(gi):
                s, o, tok0 = c_geom(gi)
                lo_c, hi_c = tok0 - 8, tok0 + 520
                if s == 0:
                    lo_v, hi_v = max(lo_c, 0), min(hi_c, 2048)
                else:
                    lo_v, hi_v = lo_c, hi_c
                if lo_v > lo_c:
                    P.op("pool", lambda e, n=lo_v - lo_c: e.memset(zp[:, :, 0:n], 0.0), writes=[zp_r])
                if hi_v < hi_c:
                    P.op("pool", lambda e, n=hi_c - hi_v: e.memset(zp[:, :, 528 - n:528], 0.0), writes=[zp_r])
                P.op("sp", lambda e, lo_v=lo_v, hi_v=hi_v, lo_c=lo_c: e.dma_start(
                    out=zp[:, :, lo_v - lo_c:hi_v - lo_c], in_=ZP[:, lo_v:hi_v].rearrange("(c p) t -> p c t", p=128)),
                    writes=[zp_r], dma=True, key=zp_r)
                P.op("sp", lambda e, s=s, o=o: e.dma_start(
                    out=invc[:], in_=invc_d[s:s + 1, :, o:o + 512].to_broadcast([128, 4, 512])),
                    writes=[invc_r], dma=True, key=invc_r)

            def mixer_pre(gi):
                s, o, tok0 = c_geom(gi)
                P.op("sp", lambda e, s=s, o=o: e.dma_start(
                    out=yaT, in_=YATT[:, s * 2048 + o:s * 2048 + o + 512].rearrange("(c p) t -> p c t", p=128)),
                    writes=list(yaT_r), dma=True, key=yaT_r[0])
                for gp in range(4):
                    zz = zp[:, 2 * gp:2 * gp + 2, :]
                    add = lambda e, o_, a, b: e.tensor_tensor(out=o_, in0=a, in1=b, op=ALU.add)
                    if gp == 0:
                        P.op("dve", lambda e, zz=zz: add(e, tB[:, :, 8:520], zz[:, :, 7:519], zz[:, :, 8:520]),
                             reads=[zp_r], writes=[tB_r])
                        wsum = tB
                    else:
                        P.op("dve", lambda e, zz=zz: add(e, tA[:, :, 1:527], zz[:, :, 0:526], zz[:, :, 1:527]),
                             reads=[zp_r], writes=[tA_r])
                        if gp == 1:
                            P.op("dve", lambda e: add(e, tB[:, :, 8:520], tA[:, :, 7:519], tA[:, :, 9:521]),
                                 reads=[tA_r], writes=[tB_r])
                            wsum = tB
                        else:
                            P.op("dve", lambda e: add(e, tB[:, :, 2:526], tA[:, :, 1:525], tA[:, :, 3:527]),
                                 reads=[tA_r], writes=[tB_r])
                            if gp == 2:
                                P.op("dve", lambda e: add(e, tA[:, :, 8:520], tB[:, :, 6:518], tB[:, :, 10:522]),
                                     reads=[tB_r], writes=[tA_r])
                                wsum = tA
                            else:
                                P.op("dve", lambda e: add(e, tA[:, :, 4:524], tB[:, :, 2:522], tB[:, :, 6:526]),
                                     reads=[tB_r], writes=[tA_r])
                                P.op("dve", lambda e: add(e, tB[:, :, 8:520], tA[:, :, 4:516], tA[:, :, 12:524]),
                                     reads=[tA_r], writes=[tB_r])
                                wsum = tB
                    w_res = tB_r if wsum is tB else tA_r
                    P.op("dve", lambda e, wsum=wsum, gp=gp: e.tensor_tensor(
                        out=wsum[:, :, 8:520], in0=wsum[:, :, 8:520],
                        in1=invc[:, gp:gp + 1, :].to_broadcast([128, 2, 512]), op=ALU.mult),
                        reads=[w_res, invc_r], writes=[w_res])
                    P.op("dve", lambda e, wsum=wsum, gp=gp, zz=zz: e.tensor_tensor(
                        out=pl[:, 2 * gp:2 * gp + 2, :], in0=wsum[:, :, 8:520], in1=zz[:, :, 8:520], op=ALU.subtract),
                        reads=[w_res, zp_r], writes=list(pl_r[2 * gp:2 * gp + 2]))

            def final_norm_tile(gi, t):
                s_, o_, _ = c_geom(gi)
                rsv, rs_res, b = rms_stats(xs[:, t, :], xs_r[t])
                P.op("dve", lambda e, rsv=rsv: e.scalar_tensor_tensor(
                    out=xs[:, t, :], in0=xs[:, t, :], scalar=rsv, in1=gfin[:], op0=ALU.mult, op1=ALU.mult),
                    reads=[xs_r[t], rs_res, gfin_r], writes=[xs_r[t]])
                r0 = s_ * 2048 + o_ + t * 128
                P.op("sp", lambda e, r0=r0: e.dma_start(out=y_out[r0:r0 + 128, :], in_=xs[:, t, :]),
                     reads=[xs_r[t]], dma=True, key=xs_r[t])

            def load_h1_tile(gi, t):
                r0 = c_tok0(gi) + t * 128
                P.op("sp", lambda e, r0=r0: e.dma_start(out=xs[:, t, :], in_=H1[r0:r0 + 128, :]),
                     writes=[xs_r[t]], dma=True, key=xs_r[t])

            mixer_loads_early(0)
            mixer_pre(0)
            for gi in range(8):
                s = gi // 4
                o = (gi % 4) * 512
                own_base = 0 if s == 0 else 2048 + 256
                tok0 = own_base + o
                if gi == 0:
                    load_u(0)
                    for t in range(4):
                        load_h1_tile(0, t)
                def pool_mix(gi=gi):
                    wpl, wpl_r, ipl = ws.get("c%dpool" % gi)
                    for gp in range(4):
                        for oc in range(2):
                            po, po_r = fbank()

                            def mmp(e, po=po, gp=gp, oc=oc, wpl=wpl):
                                for k in range(2):
                                    i = e.matmul(po, lhsT=wpl[:, gp * 2 + k, oc * 128:(oc + 1) * 128], rhs=pl[:, gp * 2 + k, :],
                                                 start=(k == 0), stop=(k == 1))
                                return i
                            P.op("pe", mmp, reads=list(pl_r[2 * gp:2 * gp + 2]) + [wpl_r], writes=[po_r])
                            ci = gp * 2 + oc
                            P.op("act", lambda e, po=po, ci=ci: e.activation(out=pmT[:, ci, :], in_=po, func=AF.Copy,
                                                                             scale=psT[:, ci:ci + 1]),
                                 reads=[po_r, psT_r], writes=[pmT_r[ci]])
                    ws.done(ipl)
                for mb in range(4):
                    for hf in range(2):
                        wga, wga_r, iga = ws.get("c%dga%d_%d" % (gi, mb, hf))
                        wgp, wgp_r, igp = ws.get("c%dgp%d_%d" % (gi, mb, hf))
                        first = (mb == 0 and hf == 0)
                        if not first:
                            (wba_, wbp_), wbb_r, ibb = ws.get("c%dbb%d_%d" % (gi, mb, hf))
                        for j2 in range(2):
                            m = mb * 4 + hf * 2 + j2
                            (p0, p0_r), (p1, p1_r) = fbank(), fbank()

                            def mm16(e, w, o, j2=j2):
                                for k in range(16):
                                    i = e.matmul(o, lhsT=w[:, k, j2 * 128:(j2 + 1) * 128], rhs=uT[:, k, :],
                                                 start=(k == 0), stop=(k == 15))
                                return i

                            def mm8(e, w, o, src, j2=j2):
                                for k in range(8):
                                    i = e.matmul(o, lhsT=w[:, k, j2 * 128:(j2 + 1) * 128], rhs=src[:, k, :],
                                                 start=(k == 0), stop=(k == 7))
                                return i
                            P.op("pe", lambda e, w=wga, o=p0, f=mm16: f(e, w, o), reads=uT_r + [wga_r], writes=[p0_r])
                            P.op("pe", lambda e, w=wgp, o=p1, f=mm16: f(e, w, o), reads=uT_r + [wgp_r], writes=[p1_r])
                            P.op("act", lambda e, p0=p0: e.activation(out=sg[:, 0, :], in_=p0, func=AF.Sigmoid),
                                 reads=[p0_r], writes=[sg_r[0]])
                            P.op("act", lambda e, p1=p1: e.activation(out=sg[:, 1, :], in_=p1, func=AF.Sigmoid),
                                 reads=[p1_r], writes=[sg_r[1]])
                            if first and j2 == 0:
                                pool_mix()
                                (wba_, wbp_), wbb_r, ibb = ws.get("c%dbb%d_%d" % (gi, mb, hf))
                            (p2, p2_r), (p3, p3_r) = fbank(), fbank()
                            P.op("pe", lambda e, w=wba_, o=p2, f=mm8: f(e, w, o, yaT), reads=list(yaT_r) + [wbb_r], writes=[p2_r])
                            P.op("pe", lambda e, w=wbp_, o=p3, f=mm8: f(e, w, o, pmT), reads=list(pmT_r) + [wbb_r], writes=[p3_r])
                            P.op("dve", lambda e, p2=p2: e.tensor_tensor(out=t1[:, 0, :], in0=sg[:, 0, :], in1=p2, op=ALU.mult),
                                 reads=[sg_r[0], p2_r], writes=[t1_r[0]])
                            P.op("dve", lambda e, p3=p3: e.tensor_tensor(out=t1[:, 1, :], in0=sg[:, 1, :], in1=p3, op=ALU.mult),
                                 reads=[sg_r[1], p3_r], writes=[t1_r[1]])
                            P.op("dve", lambda e, m=m: e.tensor_tensor(out=mT[:, m, :], in0=t1[:, 0, :], in1=t1[:, 1, :], op=ALU.add),
                                 reads=t1_r, writes=[mT_r[m]])
                            if gi > 0 and m in (1, 3, 5, 7):
                                final_norm_tile(gi - 1, m // 2)
                                load_h1_tile(gi, m // 2)
                        ws.done(iga)
                        ws.done(igp)
                        ws.done(ibb)
                P.op("dve", lambda e: e.memset(ssq[:, 0:32], 0.0), writes=[ssq_r])
                for c in range(8):
                    wo_, wo_r, iwo = ws.get("c%dwo%d" % (gi, c))
                    for t in range(4):
                        po, po_r = fbank()

                        def mmo(e, po=po, t=t, wo_=wo_):
                            for k in range(16):
                                i = e.matmul(po[:, 0:256], lhsT=mT[:, k, t * 128:(t + 1) * 128], rhs=wo_[:, k, :],
                                             start=(k == 0), stop=(k == 15))
                            return i
                        P.op("pe", mmo, reads=list(mT_r) + [wo_r], writes=[po_r])
                        P.op("dve", lambda e, po=po, t=t, c=c: e.tensor_tensor(
                            out=xs[:, t, c * 256:(c + 1) * 256], in0=po[:, 0:256], in1=xs[:, t, c * 256:(c + 1) * 256], op=ALU.add),
                            reads=[po_r, xs_r[t]], writes=[xs_r[t]])
                        partial_sq(xs[:, t, :], xs_r[t], c * 256, (c + 1) * 256, t * 8 + c)
                    ws.done(iwo)
                norm_group(hs_tiles, 2, uT, uT_r, pre=8)
                ffn(ws, "c%d" % gi, aT, aT_r, hs_tiles,
                    mid_hook=(lambda gi=gi: (load_u(gi + 1), mixer_loads_early(gi + 1))) if gi + 1 < 8 else None)
                if gi + 1 < 8:
                    mixer_pre(gi + 1)
                if gi == 7:
                    for t in range(4):
                        final_norm_tile(gi, t)
            P.barrier()
            P.emit()
        P.final_wait()
    return nc


def _bias_tables(rpb, interior=False):
    rpb = np.asarray(rpb, np.float32).reshape(16, 15, 31)
    T = np.full((8, 2, 64, 2, 22, 64), NEG, np.float32)
    qc = np.arange(64)
    cs = np.clip(qc - 8, 0, 48)
    kc = np.arange(64)
    colok = (kc[None, :] >= cs[:, None]) & (kc[None, :] < cs[:, None] + 16)
    dc = np.clip(kc[None, :] - qc[:, None] + 15, 0, 30)
    for qrl in range(2):
        for i in range(22):
            dr = i - qrl - 3
            if (3 <= dr <= 10) if interior else (0 <= dr <= 14):
                vals = rpb[:, dr, :][:, dc]
                vals = np.where(colok[None], vals, np.float32(NEG))
                T[:, qrl, :, :, i, :] = vals.reshape(8, 2, 64, 64).transpose(0, 2, 1, 3)
    return np.ascontiguousarray(T.reshape(8, 128, 2 * 22 * 64))


def _rowmask(core):
    rm = np.full((2, 12, 2048), NEG, np.float32)
    for s in range(2):
        L = 32 if s == 0 else 40
        own0 = 0 if s == 0 else 4
        for qt in range(16):
            r = own0 + 2 * qt
            nbr, kb = _band(s, qt)
            for qrl in range(2):
                if s == 0:
                    Rg, rows, koff = r + qrl, 32, 0
                else:
                    Rg, rows, koff = 32 * core - 4 + r + qrl, 256, 32 * core - 4
                rs = min(max(Rg - 4, 0), rows - 8)
                for krl in range(nbr):
                    kg = koff + kb + krl
                    if rs <= kg < rs + 8:
                        rm[s, krl, qt * 128 + qrl * 64: qt * 128 + qrl * 64 + 64] = 0.0
    return rm


def _invcnt(core):
    out = np.zeros((2, 4, 2048), np.float32)
    for s in range(2):
        T = 2048 if s == 0 else 16384
        t = np.arange(2048) + (0 if s == 0 else core * 2048)
        for gi, w in enumerate((2, 4, 8, 16)):
            lo = np.clip(t - w // 2, 0, T)
            hi = np.clip(t + w // 2, 0, T)
            out[s, gi] = 1.0 / (hi - lo).astype(np.float32)
    return out


_NC_CACHE = {}


def _prep(x_prompt, x_sample, g_ffn1, w1_gate, w1_up, w1_down, g_mix, w_in, rpb, w_pool,
          pool_scale, w_branch_attn, w_branch_pool, w_out, g_ffn2, w2_gate, w2_up, w2_down, g_final,
          cores=range(NCORES)):
    f = lambda a: np.ascontiguousarray(np.asarray(a, dtype=np.float32))
    x_prompt, x_sample = f(x_prompt), f(x_sample)
    fm = lambda g: f(g).reshape(-1, 128).T
    gT = np.ascontiguousarray(np.concatenate([fm(g_ffn1), fm(g_mix), fm(g_ffn2)], axis=1))
    psT = np.ascontiguousarray(fm(pool_scale))
    gfin = np.ascontiguousarray(np.broadcast_to(f(g_final).reshape(1, D), (128, D)))
    bias_t = _bias_tables(rpb)
    bias_i = _bias_tables(rpb, interior=True)
    e12 = np.zeros((12, 12, 64), np.float32)
    e12[np.arange(12), np.arange(12), :] = 1.0
    e12 = e12.reshape(12, 768)
    shared = {
        "w1_gate": f(w1_gate).reshape(D, DFF), "w1_up": f(w1_up).reshape(D, DFF), "w1_down": f(w1_down).reshape(DFF, D),
        "w2_gate": f(w2_gate).reshape(D, DFF), "w2_up": f(w2_up).reshape(D, DFF), "w2_down": f(w2_down).reshape(DFF, D),
        "w_in": f(w_in).reshape(D, 8192), "w_pool": f(w_pool).reshape(4, 256, 256),
        "w_branch_attn": f(w_branch_attn).reshape(1024, D), "w_branch_pool": f(w_branch_pool).reshape(1024, D),
        "w_out": f(w_out).reshape(D, D), "gT": gT, "psT": psT, "gfin": gfin, "bias_t": bias_t, "bias_i": bias_i, "e12": e12,
    }
    xs_pad = np.zeros((256 * 64 + 8 * 64, D), np.float32)
    xs_pad[256:256 + 16384] = x_sample[0]
    in_maps = []
    for c in cores:
        x_all = np.concatenate([x_prompt[c], xs_pad[c * 2048: c * 2048 + 2560]], axis=0)
        m = dict(shared)
        m["x_all"] = np.ascontiguousarray(x_all)
        m["rowmask"] = _rowmask(c)
        m["invcnt"] = _invcnt(c)
        in_maps.append(m)
    return in_maps


def kernel(**inputs):
    in_maps = _prep(**inputs)
    if "nc" not in _NC_CACHE:
        _NC_CACHE["nc"] = build_nc()
    res = run_bass_kernel_spmd(_NC_CACHE["nc"], in_maps, core_ids=list(range(NCORES)))
    y_prompt = np.empty((8, 2048, D), np.float32)
    y_sample = np.empty((1, 16384, D), np.float32)
    for c in range(NCORES):
        y = res.results[c]["y"]
        y_prompt[c] = y[:2048]
        y_sample[0, c * 2048:(c + 1) * 2048] = y[2048:]
    return (y_prompt, y_sample)
```

```python
import numpy as np
import concourse.bass as bass
import concourse.mybir as mybir
from concourse.bass_utils import run_bass_kernel_spmd

F32 = mybir.dt.float32
BF16 = mybir.dt.bfloat16
AF = mybir.ActivationFunctionType
ALU = mybir.AluOpType
AX = mybir.AxisListType

D = 2048
DFF = 5632
NJ = DFF // 128
NTOK = 4608
NOWN = 4096
NEG = -30000.0
NCORES = 8


class Res:
    __slots__ = ("name", "last_w", "readers")

    def __init__(self, name):
        self.name = name
        self.last_w = None
        self.readers = []


class Op:
    __slots__ = ("eng", "fn", "deps", "is_dma", "key", "kidx", "needed", "sig")

    def __init__(self, eng, fn, is_dma=False, key=None):
        self.eng = eng
        self.fn = fn
        self.deps = []
        self.is_dma = is_dma
        self.key = key
        self.kidx = 0
        self.needed = False
        self.sig = 0


class Prog:
    ENGS = ("pe", "act", "dve", "pool", "sp")

    def __init__(self, nc, sems):
        self.nc = nc
        self.eobj = dict(pe=nc.tensor, act=nc.scalar, dve=nc.vector, pool=nc.gpsimd, sp=nc.sync)
        self.esem = {e: sems.pop() for e in self.ENGS}
        self.free_sems = sems
        self.key_sem = {}
        self.key_cnt = {}
        self.ecnt = {e: 0 for e in self.ENGS}
        self.known = {e: {} for e in self.ENGS}
        self.ops = []
        self.last_eng_op = {e: None for e in self.ENGS}
        self.last_key_op = {}
        self.pending_barrier = None

    def _add_dep(self, op, d):
        if d is not None and d is not op:
            op.deps.append(d)

    def op(self, eng, fn, reads=(), writes=(), acc=(), dma=False, key=None):
        o = Op(eng, fn, dma, key)
        if self.pending_barrier is not None and eng not in self.pending_barrier[1]:
            for d in self.pending_barrier[0]:
                self._add_dep(o, d)
            self.pending_barrier[1].add(eng)
        for r in reads:
            self._add_dep(o, r.last_w)
        for w in writes:
            self._add_dep(o, w.last_w)
            for rd in w.readers:
                self._add_dep(o, rd)
        for r in reads:
            r.readers.append(o)
        for w in writes:
            w.last_w = o
            w.readers = []
        for w in acc:
            w.last_w = o
        if dma:
            assert key is not None
            if key not in self.key_sem:
                self.key_sem[key] = self.free_sems.pop()
                self.key_cnt[key] = 0
            self.key_cnt[key] += 1
            o.kidx = self.key_cnt[key]
            self.last_key_op[key] = o
        else:
            self.last_eng_op[eng] = o
        self.ops.append(o)
        return o

    def barrier(self):
        deps = [o for o in self.last_eng_op.values() if o is not None]
        deps += list(self.last_key_op.values())
        self.pending_barrier = (deps, set())

    def emit(self):
        for o in self.ops:
            for d in o.deps:
                d.needed = True
        for e in self.ENGS:
            if self.last_eng_op[e] is not None:
                self.last_eng_op[e].needed = True
        for o in self.ops:
            if not o.is_dma and o.needed and o.sig == 0:
                self.ecnt[o.eng] += 1
                o.sig = self.ecnt[o.eng]
        streams = {e: [] for e in self.ENGS}
        for o in self.ops:
            kn = self.known[o.eng]
            waits = []
            for d in o.deps:
                if d.is_dma:
                    s, v = self.key_sem[d.key], 16 * d.kidx
                else:
                    s, v = self.esem[d.eng], d.sig
                if kn.get(s, 0) < v:
                    kn[s] = v
                    waits.append((s, v))
            streams[o.eng].append((waits, o))
        self.ops = []
        with self.nc.Block() as block:
            for e in self.ENGS:
                items = streams[e]
                if not items:
                    continue

                def body(eng, items=items, e=e):
                    for waits, o in items:
                        best = {}
                        for s, v in waits:
                            best[s] = max(best.get(s, 0), v)
                        for s, v in best.items():
                            eng.wait_ge(s, v)
                        if o.fn is None:
                            continue
                        ins = o.fn(eng)
                        if o.is_dma:
                            ins.then_inc(self.key_sem[o.key], 16)
                        elif o.needed:
                            ins.then_inc(self.esem[e], 1)

                getattr(block, {"pe": "tensor", "act": "scalar", "dve": "vector",
                                "pool": "gpsimd", "sp": "sync"}[e])(body)

    def final_wait(self):
        items = []
        for k, s in self.key_sem.items():
            items.append((s, 16 * self.key_cnt[k]))
        with self.nc.Block() as block:
            def body(eng):
                for s, v in items:
                    eng.wait_ge(s, v)
            block.sync(body)


def _band(s, qt):
    L, own0 = (32, 0) if s == 0 else (40, 4)
    r = own0 + 2 * qt
    wide = (r >= 28) if s == 0 else (r in (4, 6, 34))
    nbr = 12 if wide else 9
    return nbr, min(max(r - 4, 0), L - nbr)


def _clipped(s, qt):
    r = (0 if s == 0 else 4) + 2 * qt
    return r in ((0, 2, 28, 30) if s == 0 else (4, 6, 32, 34))


class WStream:
    NSLOT = 6

    def __init__(self, P, ring, ring_r, sched, ngroups=1, cache=None):
        self.P, self.ring, self.ring_r, self.sched = P, ring, ring_r, sched
        self.nload = 0
        self.nget = 0
        self.free = list(range(self.NSLOT))
        self.slot_of = {}
        self.cache = cache
        assert len(sched) % ngroups == 0
        self.per_group = len(sched) // ngroups
        self.cres = [Res("wc%d" % j) for j in range(self.per_group)]

    def _pump(self):
        while self.nload < len(self.sched) and self.free:
            i = self.nload
            tag, src, nk, nc_ = self.sched[i]
            s = self.free.pop(0)
            self.slot_of[i] = s
            parts = src if isinstance(src, list) else [(src, nk, nc_)]
            ne = sum(a * b for (_, a, b) in parts)
            g, j = i // self.per_group, i % self.per_group
            rr = self.ring_r[s]
            wbg = j % 2 if self.per_group * 2 <= len(self.sched) else 0
            if self.cache is not None and g > wbg:
                self.P.op("pool", lambda e, s=s, j=j, ne=ne: e.dma_start(out=self.ring[:, s, 0:ne], in_=self.cache[j, :, 0:ne]),
                          reads=[self.cres[j]], writes=[rr], dma=True, key=rr)
            else:
                off = 0
                for (src_, nk_, ncc_) in parts:
                    dst = self.ring[:, s, off:off + nk_ * ncc_].rearrange("p (k c) -> p k c", k=nk_)
                    off += nk_ * ncc_
                    self.P.op("pool", lambda e, dst=dst, src_=src_: e.dma_start(out=dst, in_=src_),
                              writes=[rr], dma=True, key=rr)
                if self.cache is not None and g == wbg:
                    self.P.op("sp", lambda e, s=s, j=j, ne=ne: e.dma_start(out=self.cache[j, :, 0:ne], in_=self.ring[:, s, 0:ne]),
                              reads=[rr], writes=[self.cres[j]], dma=True, key=rr)
            self.nload += 1

    def get(self, tag):
        self._pump()
        i = self.nget
        t, src, nk, nc_ = self.sched[i]
        assert t == tag, (t, tag)
        assert i in self.slot_of, "weight ring exhausted"
        s = self.slot_of[i]
        self.nget += 1
        if isinstance(src, list):
            view, off = [], 0
            for (src_, nk_, ncc_) in src:
                view.append(self.ring[:, s, off:off + nk_ * ncc_].rearrange("p (k c) -> p k c", k=nk_))
                off += nk_ * ncc_
        else:
            view = self.ring[:, s, 0:nk * nc_].rearrange("p (k c) -> p k c", k=nk)
        return view, self.ring_r[s], i

    def done(self, i):
        self.free.append(self.slot_of.pop(i))
        self._pump()


def build_nc(dbg=False):
    nc = bass.Bass("TRN2", target_bir_lowering=False)
    ein = lambda n, shp: nc.dram_tensor(n, shp, F32, kind="ExternalInput").ap()
    x_all = ein("x_all", [NTOK, D])
    w1g, w1u, w1d = ein("w1_gate", [D, DFF]), ein("w1_up", [D, DFF]), ein("w1_down", [DFF, D])
    w2g, w2u, w2d = ein("w2_gate", [D, DFF]), ein("w2_up", [D, DFF]), ein("w2_down", [DFF, D])
    w_in = ein("w_in", [D, 8192])
    w_pool = ein("w_pool", [4, 256, 256])
    w_ba, w_bp = ein("w_branch_attn", [1024, D]), ein("w_branch_pool", [1024, D])
    w_out = ein("w_out", [D, D])
    gT_d = ein("gT", [128, 48])
    psT_d = ein("psT", [128, 8])
    gfin_d = ein("gfin", [128, D])
    bias_d = ein("bias_t", [8, 128, 2 * 22 * 64])
    biasi_d = ein("bias_i", [8, 128, 2 * 22 * 64])
    rm_d = ein("rowmask", [2, 12, 2048])
    e12_d = ein("e12", [12, 768])
    invc_d = ein("invcnt", [2, 4, 2048])
    y_out = nc.dram_tensor("y", [NOWN, D], F32, kind="ExternalOutput").ap()
    skind = "ExternalOutput" if dbg else "Internal"
    H1 = nc.dram_tensor("H1", [NTOK, D], F32, kind=skind).ap()
    QKV = nc.dram_tensor("QKV", [3072, NTOK], BF16, kind=skind).ap()
    ZP = nc.dram_tensor("ZP", [1024, NTOK], F32, kind=skind).ap()
    YATT = nc.dram_tensor("YATT", [1024, NOWN], BF16, kind=skind).ap()
    U2T = nc.dram_tensor("U2T", [D, NTOK], BF16, kind=skind).ap()
    WCA = nc.dram_tensor("WCA", [84, 128, 4096], BF16).ap()
    WCC = nc.dram_tensor("WCC", [101, 128, 4096], BF16).ap()

    from contextlib import ExitStack
    with ExitStack() as es:
        sems = [es.enter_context(nc.semaphore("s%d" % i)) for i in range(48)]
        P = Prog(nc, sems)
        sb = lambda n, shp, dt: es.enter_context(nc.sbuf_tensor("k_" + n, shp, dt))
        ident = sb("ident", [128, 128], BF16)
        identf = sb("identf", [128, 128], F32)
        gT = sb("gT_sb", [128, 48], F32)
        psT = sb("psT_sb", [128, 8], F32)
        st_ss = sb("st_ss", [128, 8], F32)
        st_rs = sb("st_rs", [128, 8], F32)
        ssq = sb("ssq", [128, 32], F32)
        pT = es.enter_context(nc.psum_tensor("pT", [128, 2, 1024], BF16))
        pF = es.enter_context(nc.psum_tensor("pF", [128, 6, 512], F32))

        R = Res
        ident_r, gT_r, psT_r = R("ident"), R("gT"), R("psT")
        xs_r = [R("xs%d" % i) for i in range(4)]
        uT_r = [R("uT%d" % i) for i in range(4)]
        un_r = [R("un%d" % i) for i in range(4)]
        ring_r = [R("ring%d" % i) for i in range(6)]
        sg_r = [R("sg%d" % i) for i in range(2)]
        ss_r = [R("ss%d" % i) for i in range(8)]
        ss4_r = [R("ss4_%d" % i) for i in range(2)]
        ssq_r = R("ssq")
        rs_r = [R("rs%d" % i) for i in range(8)]
        pT_r = [R("pT%d" % i) for i in range(2)]
        pF_r = [R("pF%d" % i) for i in range(6)]
        cnt = {"pT": 0, "pF": 0, "un": 0, "st": 0, "sg": 0, "ev": 0, "st4": 0}

        def nxt(k, n):
            v = cnt[k] % n
            cnt[k] += 1
            return v

        def fbank():
            b = nxt("pF", 6)
            return pF[:, b, :], pF_r[b]

        P.op("pool", lambda e: e.memset(identf[:], 0.0), writes=[ident_r])
        P.op("pool", lambda e: e.affine_select(out=identf[:], in_=identf[:], pattern=[[-1, 128]],
                                               compare_op=ALU.not_equal, fill=1.0, base=0,
                                               channel_multiplier=1), writes=[ident_r])
        P.op("dve", lambda e: e.tensor_copy(out=ident[:], in_=identf[:]), writes=[ident_r])
        P.op("sp", lambda e: e.dma_start(out=gT[:], in_=gT_d[:, :]), writes=[gT_r], dma=True, key=gT_r)
        P.op("sp", lambda e: e.dma_start(out=psT[:], in_=psT_d[:, :]), writes=[psT_r], dma=True, key=psT_r)

        def rms_stats(src, src_r):
            b = nxt("un", 4)
            s = nxt("st", 8)
            ssv, rsv = st_ss[:, s:s + 1], st_rs[:, s:s + 1]
            P.op("dve", lambda e: e.memset(ssv, 0.0), writes=[ss_r[s]])
            P.op("act", lambda e: e.activation(out=un[:, b, :], in_=src, func=AF.Square,
                                               scale=float(D ** -0.5), accum_out=ssv),
                 reads=[src_r], writes=[un_r[b], ss_r[s]])
            P.op("dve", lambda e: e.tensor_scalar_add(out=ssv, in0=ssv, scalar1=1e-6),
                 reads=[ss_r[s]], writes=[ss_r[s]])
            P.op("act", lambda e: e.activation(out=rsv, in_=ssv, func=AF.Sqrt),
                 reads=[ss_r[s]], writes=[rs_r[s]])
            P.op("dve", lambda e: e.reciprocal(out=rsv, in_=rsv), reads=[rs_r[s]], writes=[rs_r[s]])
            return rsv, rs_r[s], b

        def norm_transpose(src, src_r, gidx, t):
            rsv, rs_res, b = rms_stats(src, src_r)
            P.op("act", lambda e: e.activation(out=un[:, b, :], in_=src, func=AF.Copy, scale=rsv),
                 reads=[src_r, rs_res], writes=[un_r[b]])
            for h in range(2):
                pb = nxt("pT", 2)

                def tr(e, h=h, pb=pb):
                    for kk in range(8):
                        k = h * 8 + kk
                        i = e.transpose(out=pT[:, pb, kk * 128:(kk + 1) * 128],
                                        in_=un[:, b, k * 128:(k + 1) * 128], identity=ident[:])
                    return i
                P.op("pe", tr, reads=[un_r[b], ident_r], writes=[pT_r[pb]])
                gsl = gT[:, gidx * 16 + h * 8: gidx * 16 + h * 8 + 8]
                P.op("dve", lambda e, h=h, pb=pb, gsl=gsl: e.tensor_tensor(
                    out=uT[:, h * 8:(h + 1) * 8, t * 128:(t + 1) * 128],
                    in0=pT[:, pb, :].rearrange("p (k c) -> p k c", k=8),
                    in1=gsl.unsqueeze(2).to_broadcast([128, 8, 128]), op=ALU.mult),
                    reads=[pT_r[pb], gT_r], writes=[uT_r[t]])

        def norm_scale(tiles, pre=None):
            g4 = nxt("st4", 2)
            ss4, rs4 = st_ss[:, g4 * 4:g4 * 4 + 4], st_rs[:, g4 * 4:g4 * 4 + 4]
            res4 = ss4_r[g4]
            if pre is not None:
                P.op("dve", lambda e: e.tensor_reduce(out=ss4, in_=ssq[:, 0:4 * pre].rearrange("p (t c) -> p t c", c=pre),
                                                      axis=AX.X, op=ALU.add), reads=[ssq_r], writes=[res4])
            else:
                P.op("dve", lambda e: e.memset(ss4, 0.0), writes=[res4])
                for t, (src, src_r) in enumerate(tiles):
                    P.op("act", lambda e, t=t, src=src: e.activation(out=un[:, t, :], in_=src, func=AF.Square,
                                                                     scale=float(D ** -0.5), accum_out=ss4[:, t:t + 1]),
                         reads=[src_r, res4], writes=[un_r[t], res4])
            P.op("dve", lambda e: e.tensor_scalar_add(out=ss4, in0=ss4, scalar1=1e-6), reads=[res4], writes=[res4])
            P.op("act", lambda e: e.activation(out=rs4, in_=ss4, func=AF.Sqrt), reads=[res4], writes=[res4])
            P.op("dve", lambda e: e.reciprocal(out=rs4, in_=rs4), reads=[res4], writes=[res4])
            for t, (src, src_r) in enumerate(tiles):
                if t % 2 == 0:
                    P.op("act", lambda e, t=t, src=src: e.activation(out=un[:, t, :], in_=src, func=AF.Copy,
                                                                     scale=rs4[:, t:t + 1]),
                         reads=[src_r, res4], writes=[un_r[t]])
                else:
                    P.op("dve", lambda e, t=t, src=src: e.tensor_scalar(out=un[:, t, :], in0=src, scalar1=rs4[:, t:t + 1],
                                                                        scalar2=None, op0=ALU.mult),
                         reads=[src_r, res4], writes=[un_r[t]])

        def norm_tr(gidx, uTd, uTd_r):
            for t in range(4):
                for h in range(2):
                    pb = nxt("pT", 2)

                    def tr(e, h=h, pb=pb, t=t):
                        for kk in range(8):
                            k = h * 8 + kk
                            i = e.transpose(out=pT[:, pb, kk * 128:(kk + 1) * 128],
                                            in_=un[:, t, k * 128:(k + 1) * 128], identity=ident[:])
                        return i
                    P.op("pe", tr, reads=[un_r[t], ident_r], writes=[pT_r[pb]])
                    gsl = gT[:, gidx * 16 + h * 8: gidx * 16 + h * 8 + 8]
                    P.op("dve", lambda e, h=h, pb=pb, gsl=gsl, t=t: e.tensor_tensor(
                        out=uTd[:, h * 8:(h + 1) * 8, t * 128:(t + 1) * 128],
                        in0=pT[:, pb, :].rearrange("p (k c) -> p k c", k=8),
                        in1=gsl.unsqueeze(2).to_broadcast([128, 8, 128]), op=ALU.mult),
                        reads=[pT_r[pb], gT_r], writes=[uTd_r[t]])

        def norm_group(tiles, gidx, uTd, uTd_r, pre=None):
            norm_scale(tiles, pre)
            norm_tr(gidx, uTd, uTd_r)

        def partial_sq(hv, hr, c0, c1, col):
            P.op("act", lambda e: e.activation(out=sg[:, 0, 0:c1 - c0], in_=hv[:, c0:c1], func=AF.Square,
                                               scale=float(D ** -0.5), accum_out=ssq[:, col:col + 1]),
                 reads=[hr, ssq_r], writes=[sg_r[0], ssq_r])

        DQ = [(0, 8), (8, 8), (16, 8), (24, 8), (32, 8), (40, 4)]

        def ffn_sched(wg, wu, wd, name):
            s = []
            wgv = wg.rearrange("(k p) c -> p k c", p=128)
            wuv = wu.rearrange("(k p) c -> p k c", p=128)
            wdv = wd.rearrange("(j p) c -> p j c", p=128)
            for jb in range(22):
                s.append((name + "g%d" % jb, wgv[:, :, jb * 256:(jb + 1) * 256], 16, 256))
                s.append((name + "u%d" % jb, wuv[:, :, jb * 256:(jb + 1) * 256], 16, 256))
            for c in range(4):
                for qi, (q0, qn) in enumerate(DQ):
                    s.append((name + "d%d_%d" % (c, qi), wdv[:, q0:q0 + qn, c * 512:(c + 1) * 512], qn, 512))
            return s

        def ffn(ws, name, aT, aT_r, hs_tiles, uT=None, uT_r=None, mid_hook=None, stats=False):
            uT, uT_r = (uT_main, uT_main_r) if uT is None else (uT, uT_r)
            for jb in range(22):
                wgs, wg_r, ig = ws.get(name + "g%d" % jb)
                wus, wu_r, iu = ws.get(name + "u%d" % jb)
                for j in range(2):
                    jj = jb * 2 + j
                    pg, pg_r = fbank()
                    pu, pu_r = fbank()

                    def mmg(e, w=wgs, o=pg, j=j):
                        for k in range(16):
                            i = e.matmul(o, lhsT=w[:, k, j * 128:(j + 1) * 128], rhs=uT[:, k, :],
                                         start=(k == 0), stop=(k == 15))
                        return i
                    P.op("pe", mmg, reads=uT_r + [wg_r], writes=[pg_r])
                    P.op("pe", lambda e, w=wus, o=pu, j=j, f=mmg: f(e, w, o, j), reads=uT_r + [wu_r], writes=[pu_r])
                    sb_ = nxt("sg", 2)
                    P.op("act", lambda e, sb_=sb_, pg=pg: e.activation(out=sg[:, sb_, :], in_=pg, func=AF.Silu),
                         reads=[pg_r], writes=[sg_r[sb_]])
                    P.op("dve", lambda e, sb_=sb_, pu=pu, jj=jj: e.tensor_tensor(
                        out=aT[:, jj, :], in0=sg[:, sb_, :], in1=pu, op=ALU.mult),
                        reads=[sg_r[sb_], pu_r], writes=[aT_r[jj]])
                ws.done(ig)
                ws.done(iu)
            if mid_hook is not None:
                mid_hook()
            if stats:
                P.op("dve", lambda e: e.memset(ssq[:, 0:16], 0.0), writes=[ssq_r])
            for c in range(4):
                banks = [fbank() for _ in range(4)]
                for qi, (q0, qn) in enumerate(DQ):
                    wds, wd_r, idd = ws.get(name + "d%d_%d" % (c, qi))
                    for t in range(4):
                        pd, pd_r = banks[t]

                        def mmd(e, w=wds, o=pd, q0=q0, qn=qn, t=t):
                            for j in range(qn):
                                i = e.matmul(o, lhsT=aT[:, q0 + j, t * 128:(t + 1) * 128], rhs=w[:, j, :],
                                             start=(q0 + j == 0), stop=(q0 + j == NJ - 1))
                            return i
                        rd = aT_r[q0:q0 + qn] + [wd_r]
                        if qi == 0:
                            P.op("pe", mmd, reads=rd, writes=[pd_r])
                        else:
                            P.op("pe", mmd, reads=rd, acc=[pd_r])
                    ws.done(idd)
                for t in range(4):
                    pd, pd_r = banks[t]
                    hv, hr = hs_tiles[t]
                    P.op("dve", lambda e, pd=pd, hv=hv, c=c: e.scalar_tensor_tensor(
                        out=hv[:, c * 512:(c + 1) * 512], in0=pd, scalar=0.5, in1=hv[:, c * 512:(c + 1) * 512],
                        op0=ALU.mult, op1=ALU.add), reads=[pd_r, hr], writes=[hr])
                    if stats:
                        partial_sq(hv, hr, c * 512, (c + 1) * 512, t * 4 + c)

        w_in_v = w_in.rearrange("(k p) c -> p k c", p=128)

        with ExitStack() as pa:
            xs = pa.enter_context(nc.sbuf_tensor("a_xs", [128, 4, D], F32))
            uT = pa.enter_context(nc.sbuf_tensor("a_uT", [128, 16, 512], BF16))
            un = pa.enter_context(nc.sbuf_tensor("a_un", [128, 4, D], BF16))
            ring = pa.enter_context(nc.sbuf_tensor("a_ring", [128, 6, 4096], BF16))
            sg = pa.enter_context(nc.sbuf_tensor("a_sg", [128, 2, 512], F32))
            hs_tiles = [(xs[:, t, :], xs_r[t]) for t in range(4)]
            uT_main, uT_main_r = uT, uT_r
            aT = pa.enter_context(nc.sbuf_tensor("aT_a", [128, NJ, 512], BF16))
            stgb = pa.enter_context(nc.sbuf_tensor("stgb", [128, 2, 4, 512], BF16))
            stgf = pa.enter_context(nc.sbuf_tensor("stgf", [128, 2, 4, 512], F32))
            aT_r = [R("aT%d" % j) for j in range(NJ)]
            stgb_r = [R("stgb%d" % i) for i in range(2)]
            stgf_r = [R("stgf%d" % i) for i in range(2)]
            NGA = NTOK // 512
            sched = []
            for gi in range(NGA):
                sched += ffn_sched(w1g, w1u, w1d, "a%d" % gi)
                for nb in range(16):
                    sched.append(("a%din%d" % (gi, nb), w_in_v[:, :, nb * 256:(nb + 1) * 256], 16, 256))
            ws = WStream(P, ring, ring_r, sched, ngroups=NGA, cache=WCA)
            ws._pump()
            uT2 = pa.enter_context(nc.sbuf_tensor("k_uT2", [128, 16, 512], BF16))
            uT2_r = [R("uT2_%d" % i) for i in range(4)]

            def load_x(gi):
                for t in range(4):
                    r0 = gi * 512 + t * 128
                    P.op("sp", lambda e, t=t, r0=r0: e.dma_start(out=xs[:, t, :], in_=x_all[r0:r0 + 128, :]),
                         writes=[xs_r[t]], dma=True, key=xs_r[t])
            load_x(0)
            norm_group(hs_tiles, 0, uT, uT_r)
            for gi in range(NGA):
                tok0 = gi * 512
                ffn(ws, "a%d" % gi, aT, aT_r, hs_tiles, stats=True)
                for t in range(4):
                    r0 = tok0 + t * 128
                    P.op("sp", lambda e, t=t, r0=r0: e.dma_start(out=H1[r0:r0 + 128, :], in_=xs[:, t, :]),
                         reads=[xs_r[t]], dma=True, key=xs_r[t])
                norm_group(hs_tiles, 1, uT2, uT2_r, pre=4)
                P.op("sp", lambda e, tok0=tok0: e.dma_start(
                    out=U2T[:, tok0:tok0 + 512].rearrange("(k p) t -> p k t", p=128), in_=uT2[:]),
                    reads=uT2_r, dma=True, key=uT2_r[0])
                if gi + 1 < NGA:
                    load_x(gi + 1)
                for nb in range(8):
                    sp_ = nb % 2
                    isz = nb >= 6
                    stg, stg_res = (stgf, stgf_r[sp_]) if isz else (stgb, stgb_r[sp_])
                    for half in range(2):
                        wv, w_r, iw = ws.get("a%din%d" % (gi, nb * 2 + half))
                        for j2 in range(2):
                            j = half * 2 + j2
                            po, po_r = fbank()

                            def mmi(e, w=wv, o=po, j2=j2):
                                for k in range(16):
                                    i = e.matmul(o, lhsT=w[:, k, j2 * 128:(j2 + 1) * 128], rhs=uT2[:, k, :],
                                                 start=(k == 0), stop=(k == 15))
                                return i
                            P.op("pe", mmi, reads=uT2_r + [w_r], writes=[po_r])
                            dst = stg[:, sp_, j, :]
                            if nb < 2:
                                P.op("act", lambda e, dst=dst, po=po: e.activation(out=dst, in_=po, func=AF.Copy, scale=0.125),
                                     reads=[po_r], writes=[stg_res])
                            elif nxt("ev", 2) == 0:
                                P.op("act", lambda e, dst=dst, po=po: e.activation(out=dst, in_=po, func=AF.Copy),
                                     reads=[po_r], writes=[stg_res])
                            else:
                                P.op("dve", lambda e, dst=dst, po=po: e.tensor_copy(out=dst, in_=po),
                                     reads=[po_r], writes=[stg_res])
                        ws.done(iw)
                    if isz:
                        dstd = ZP[(nb - 6) * 512:(nb - 5) * 512, tok0:tok0 + 512].rearrange("(j p) t -> p j t", p=128)
                    else:
                        dstd = QKV[nb * 512:(nb + 1) * 512, tok0:tok0 + 512].rearrange("(j p) t -> p j t", p=128)
                    P.op("sp", lambda e, dstd=dstd, stg=stg, sp_=sp_: e.dma_start(out=dstd, in_=stg[:, sp_]),
                         reads=[stg_res], dma=True, key=stg_res)
                    if nb == 1 and gi + 1 < NGA:
                        norm_scale(hs_tiles)
                    if nb == 4 and gi + 1 < NGA:
                        norm_tr(0, uT, uT_r)
            P.barrier()
            P.emit()

        with ExitStack() as pb_:
            sbb = lambda n, shp, dt: pb_.enter_context(nc.sbuf_tensor("b_" + n, shp, dt))
            NSB = 3
            qz = sbb("qz", [128, 2, 2, 2048], BF16)
            kT = sbb("kT", [128, 2, 2560], BF16)
            vT = sbb("vT", [128, 2, 2560], BF16)
            Vp = sbb("Vp", [128, 2, 20, 2, 128], BF16)
            T2 = sbb("T2", [128, 2, 2, 22, 64], F32)
            T2i = sbb("T2i", [128, 2, 2, 22, 64], F32)
            rmT = sbb("rmT", [12, 2, 2048], BF16)
            e12 = sbb("e12", [12, 768], BF16)
            Sb = sbb("Sb", [128, NSB, 768], F32)
            Pf = sbb("Pf", [128, NSB, 768], F32)
            Pn = sbb("Pn", [128, NSB, 768], BF16)
            PT = sbb("PT", [128, 4, 6, 128], BF16)
            yT = sbb("yT", [128, 2, 2048], BF16)
            mx = sbb("mx", [128, 8], F32)
            sm = sbb("sm", [128, 8], F32)
            qT_r = [R("qT%d" % i) for i in range(2)]
            kT_r = [R("kT%d" % i) for i in range(2)]
            vT_r = [R("vT%d" % i) for i in range(2)]
            Vp_r = [R("Vp%d" % i) for i in range(2)]
            T2_r = [R("T2%d" % i) for i in range(2)]
            rm_r, rmf_r, e12_r = R("rm"), R("rmf"), R("e12")
            Sb_r = [R("Sb%d" % i) for i in range(NSB)]
            Pf_r = [R("Pf%d" % i) for i in range(NSB)]
            Pn_r = [R("Pn%d" % i) for i in range(NSB)]
            PT_r = [R("PT%d" % i) for i in range(4)]
            yT_r = [R("yT%d" % i) for i in range(2)]
            mx_r = [R("mx%d" % i) for i in range(8)]
            sm_r = [R("sm%d" % i) for i in range(8)]
            P.op("dve", lambda e: e.memset(Vp[:], 0.0), writes=Vp_r)
            P.op("dve", lambda e: e.memset(qz[:], 0.0), writes=qT_r)
            for s in range(2):
                P.op("pool", lambda e, s=s: e.dma_start(out=rmT[:, s, :], in_=rm_d[s]), writes=[rm_r], dma=True, key=rm_r)
            P.op("pool", lambda e: e.dma_start(out=e12[:], in_=e12_d[:, :]), writes=[e12_r], dma=True, key=e12_r)

            SEQ = [(32, 0, 0), (40, 2048, 4)]
            NBLK = 16
            NIT = 32

            def b_load(bi):
                s, hp, hb = bi // 8, bi % 8, bi % 2
                L, tokb, own0 = SEQ[s]
                q0 = tokb + own0 * 64
                for eh in range(2):
                    P.op("sp", lambda e, eh=eh: e.dma_start(
                        out=qz[eh * 64:(eh + 1) * 64, hb, eh, :],
                        in_=QKV[hp * 128 + eh * 64:hp * 128 + (eh + 1) * 64, q0:q0 + 2048]),
                        writes=[qT_r[hb]], dma=True, key=qT_r[hb])
                P.op("sp", lambda e: e.dma_start(
                    out=kT[:, hb, 0:L * 64], in_=QKV[1024 + hp * 128:1024 + (hp + 1) * 128, tokb:tokb + L * 64]),
                    writes=[kT_r[hb]], dma=True, key=kT_r[hb])
                P.op("sp", lambda e: e.dma_start(
                    out=vT[:, hb, 0:L * 64], in_=QKV[2048 + hp * 128:2048 + (hp + 1) * 128, tokb:tokb + L * 64]),
                    writes=[vT_r[hb]], dma=True, key=vT_r[hb])
                P.op("sp", lambda e: e.dma_start(out=T2[:, hb], in_=bias_d[hp].rearrange("p (e i c) -> p e i c", e=2, i=22)),
                     writes=[T2_r[hb]], dma=True, key=T2_r[hb])
                P.op("sp", lambda e: e.dma_start(out=T2i[:, hb], in_=biasi_d[hp].rearrange("p (e i c) -> p e i c", e=2, i=22)),
                     writes=[T2_r[hb]], dma=True, key=T2_r[hb])

            def b_vtrans(bi, pb):
                s, hb = bi // 8, bi % 2
                nkt = SEQ[s][0] // 2
                for k0 in range(0, nkt, 8):
                    n = min(8, nkt - k0)

                    def trv(e, k0=k0, n=n, pb=pb):
                        for kk in range(n):
                            i = e.transpose(out=pT[:, pb, kk * 128:(kk + 1) * 128],
                                            in_=vT[:, hb, (k0 + kk) * 128:(k0 + kk + 1) * 128], identity=ident[:])
                        return i
                    P.op("pe", trv, reads=[vT_r[hb], ident_r], writes=[pT_r[pb]])
                    pv = pT[:, pb, 0:n * 128].rearrange("p (k c) -> p k c", k=n)
                    P.op("dve", lambda e, k0=k0, n=n, pv=pv: e.tensor_copy(out=Vp[:, hb, k0:k0 + n, 0, 0:64], in_=pv[:, :, 0:64]),
                         reads=[pT_r[pb]], writes=[Vp_r[hb]])
                    P.op("act", lambda e, k0=k0, n=n, pv=pv: e.activation(out=Vp[:, hb, k0:k0 + n, 1, 64:128], in_=pv[:, :, 64:128], func=AF.Copy),
                         reads=[pT_r[pb]], writes=[Vp_r[hb]])

            def geom(n):
                bi, loc = n // NIT, n % NIT
                s, hb = bi // 8, bi % 2
                L, tokb, own0 = SEQ[s]
                qt, eh = loc // 2, loc % 2
                r = own0 + 2 * qt
                nbr, kb = _band(s, qt)
                return dict(s=s, hb=hb, qt=qt, eh=eh, kb=kb, nbr=nbr, i0=kb - r + 10, kt0=kb // 2, clip=_clipped(s, qt),
                            ncol=nbr * 64, nT=(nbr + 1) // 2)

            def stage_S(n):
                g = geom(n)
                s, hb, qt, kb, nbr = g["s"], g["hb"], g["qt"], g["kb"], g["nbr"]
                lo, hi = g["eh"] * 64, g["eh"] * 64 + 64
                ba, bb = (n % 2) * 2, (n % 2) * 2 + 1
                pa_, pbk = pF[:, ba, :], pF[:, bb, :]
                nB = g["ncol"] - 512

                eh, clip = g["eh"], g["clip"]

                def mms(e):
                    ql = qz[:, hb, eh, qt * 128:(qt + 1) * 128]
                    e.matmul(pa_, lhsT=ql, rhs=kT[:, hb, kb * 64:kb * 64 + 512], start=True, stop=not clip)
                    if clip:
                        e.matmul(pa_, lhsT=rmT[0:nbr, s, qt * 128:(qt + 1) * 128], rhs=e12[0:nbr, 0:512],
                                 start=False, stop=True)
                    i = e.matmul(pbk[:, 0:nB], lhsT=ql, rhs=kT[:, hb, kb * 64 + 512:kb * 64 + 512 + nB],
                                 start=True, stop=not clip)
                    if clip:
                        i = e.matmul(pbk[:, 0:nB], lhsT=rmT[0:nbr, s, qt * 128:(qt + 1) * 128], rhs=e12[0:nbr, 512:512 + nB],
                                     start=False, stop=True)
                    return i
                P.op("pe", mms, reads=[qT_r[hb], kT_r[hb], rm_r, e12_r], writes=[pF_r[ba], pF_r[bb]])

            def stage_bias(n):
                g = geom(n)
                hb, eh, i0, nbr, ncol = g["hb"], g["eh"], g["i0"], g["nbr"], g["ncol"]
                ba, bb = (n % 2) * 2, (n % 2) * 2 + 1
                pa_, pbk = pF[:, ba, :], pF[:, bb, :]
                sbi, si = n % NSB, n % 8
                mxv, smv = mx[:, si:si + 1], sm[:, si:si + 1]
                TT = T2 if g["clip"] else T2i
                P.op("dve", lambda e: e.tensor_tensor(
                    out=Sb[:, sbi, 0:512], in0=pa_, in1=TT[:, hb, eh, i0:i0 + 8, :].rearrange("p i c -> p (i c)"),
                    op=ALU.add), reads=[pF_r[ba], T2_r[hb]], writes=[Sb_r[sbi]])
                P.op("dve", lambda e: e.tensor_tensor(
                    out=Sb[:, sbi, 512:ncol], in0=pbk[:, 0:ncol - 512],
                    in1=TT[:, hb, eh, i0 + 8:i0 + nbr, :].rearrange("p i c -> p (i c)"),
                    op=ALU.add), reads=[pF_r[bb], T2_r[hb], Sb_r[sbi]], writes=[Sb_r[sbi]])
                P.op("dve", lambda e: e.reduce_max(out=mxv, in_=Sb[:, sbi, 0:ncol], axis=AX.X),
                     reads=[Sb_r[sbi]], writes=[mx_r[si]])
                P.op("dve", lambda e: e.tensor_scalar_mul(out=mxv, in0=mxv, scalar1=-1.0),
                     reads=[mx_r[si]], writes=[mx_r[si]])
                P.op("dve", lambda e: e.memset(smv, 0.0), writes=[sm_r[si]])

            def stage_exp(n):
                ncol = geom(n)["ncol"]
                sbi, si = n % NSB, n % 8
                mxv, smv = mx[:, si:si + 1], sm[:, si:si + 1]
                P.op("act", lambda e: e.activation(
                    out=Pf[:, sbi, 0:ncol], in_=Sb[:, sbi, 0:ncol], func=AF.Exp, bias=mxv, scale=1.0, accum_out=smv),
                    reads=[Sb_r[sbi], mx_r[si]], writes=[Pf_r[sbi], sm_r[si]])

            def stage_recip(n):
                si = n % 8
                smv = sm[:, si:si + 1]
                P.op("dve", lambda e: e.reciprocal(out=smv, in_=smv), reads=[sm_r[si]], writes=[sm_r[si]])

            def stage_norm(n):
                ncol = geom(n)["ncol"]
                sbi, si = n % NSB, n % 8
                smv = sm[:, si:si + 1]
                P.op("act", lambda e: e.activation(
                    out=Pn[:, sbi, 0:ncol], in_=Pf[:, sbi, 0:ncol], func=AF.Copy, scale=smv),
                    reads=[Pf_r[sbi], sm_r[si]], writes=[Pn_r[sbi]])

            def stage_T(n):
                g = geom(n)
                nT, ncol = g["nT"], g["ncol"]
                sbi, pb = n % NSB, n % 2

                def trp(e):
                    for kk in range(nT):
                        w = min(128, ncol - kk * 128)
                        i = e.transpose(out=pT[0:w, pb, kk * 128:(kk + 1) * 128],
                                        in_=Pn[:, sbi, kk * 128:kk * 128 + w], identity=ident[:])
                    return i
                P.op("pe", trp, reads=[Pn_r[sbi], ident_r], writes=[pT_r[pb]])

            def stage_copy(n):
                nT = geom(n)["nT"]
                pb, pti = n % 2, n % 4
                P.op("act", lambda e: e.activation(
                    out=PT[:, pti, 0:nT, :], in_=pT[:, pb, 0:nT * 128].rearrange("p (k c) -> p k c", k=nT), func=AF.Copy),
                    reads=[pT_r[pb]], writes=[PT_r[pti]])

            def stage_PV(n):
                g = geom(n)
                hb, qt, kt0, nT, ncol = g["hb"], g["qt"], g["kt0"], g["nT"], g["ncol"]
                yb_ = 4 + (qt % 2)
                py = pF[:, yb_, :]

                def mmy(e):
                    for eh_ in range(2):
                        pti = (n - 1 + eh_) % 4
                        for kk in range(nT):
                            w = min(128, ncol - kk * 128)
                            i = e.matmul(py[:, 0:128], lhsT=Vp[0:w, hb, kt0 + kk, eh_, :], rhs=PT[0:w, pti, kk, :],
                                         start=(eh_ == 0 and kk == 0), stop=(eh_ == 1 and kk == nT - 1))
                    return i
                P.op("pe", mmy, reads=[Vp_r[hb], PT_r[(n - 1) % 4], PT_r[n % 4]], writes=[pF_r[yb_]])

            def stage_y(n):
                g = geom(n)
                hb, qt = g["hb"], g["qt"]
                yb_ = 4 + (qt % 2)
                py = pF[:, yb_, :]
                P.op("dve", lambda e: e.tensor_copy(out=yT[:, hb, qt * 128:(qt + 1) * 128], in_=py[:, 0:128]),
                     reads=[pF_r[yb_]], writes=[yT_r[hb]])
                if qt == 15:
                    bi = n // NIT
                    s, hp = bi // 8, bi % 8
                    P.op("sp", lambda e: e.dma_start(
                        out=YATT[hp * 128:(hp + 1) * 128, s * 2048:(s + 1) * 2048], in_=yT[:, hb, :]),
                        reads=[yT_r[hb]], dma=True, key=yT_r[hb])

            b_load(0)
            b_vtrans(0, 0)
            NTOT = NBLK * NIT
            for step in range(NTOT + 8):
                if step < NTOT and step % NIT == 8 and step // NIT + 1 < NBLK:
                    b_load(step // NIT + 1)
                if step < NTOT:
                    stage_S(step)
                if 0 <= step - 3 < NTOT:
                    stage_recip(step - 3)
                if 0 <= step - 1 < NTOT:
                    stage_bias(step - 1)
                if 0 <= step - 2 < NTOT:
                    stage_exp(step - 2)
                if 0 <= step - 3 < NTOT:
                    stage_norm(step - 3)
                if 0 <= step - 4 < NTOT:
                    stage_T(step - 4)
                if 0 <= step - 5 < NTOT:
                    stage_copy(step - 5)
                if 0 <= step - 6 < NTOT and (step - 6) % 2 == 1:
                    stage_PV(step - 6)
                if 0 <= step - 7 < NTOT and (step - 7) % 2 == 1:
                    stage_y(step - 7)
                if step < NTOT and step % NIT == 16 and step // NIT + 1 < NBLK:
                    b_vtrans(step // NIT + 1, (step - 5) % 2)
            P.barrier()
            P.emit()

        with ExitStack() as pc:
            xs = pc.enter_context(nc.sbuf_tensor("c_xs", [128, 4, D], F32))
            uT = pc.enter_context(nc.sbuf_tensor("c_uT", [128, 16, 512], BF16))
            un = pc.enter_context(nc.sbuf_tensor("c_un", [128, 4, D], BF16))
            ring = pc.enter_context(nc.sbuf_tensor("c_ring", [128, 6, 4096], BF16))
            sg = pc.enter_context(nc.sbuf_tensor("c_sg", [128, 2, 512], F32))
            hs_tiles = [(xs[:, t, :], xs_r[t]) for t in range(4)]
            uT_main, uT_main_r = uT, uT_r
            sbc = lambda n, shp, dt: pc.enter_context(nc.sbuf_tensor("c_" + n, shp, dt))
            aT = sbc("aT_c", [128, NJ, 512], BF16)
            aT_r = [R("aTc%d" % j) for j in range(NJ)]
            zp = sbc("zp", [128, 8, 528], F32)
            tA = sbc("tA", [128, 2, 528], F32)
            tB = sbc("tB", [128, 2, 528], F32)
            invc = sbc("invc", [128, 4, 512], F32)
            gfin = sbc("gfin", [128, D], F32)
            t1 = sbc("t1", [128, 2, 512], F32)
            zp_r, tA_r, tB_r, invc_r, gfin_r = R("zp"), R("tA"), R("tB"), R("invc"), R("gfin")
            t1_r = [R("t1_%d" % i) for i in range(2)]
            mT, mT_r = aT[:, 0:16], aT_r[0:16]
            yaT, yaT_r = aT[:, 16:24], aT_r[16:24]
            pmT, pmT_r = aT[:, 24:32], aT_r[24:32]
            pl, pl_r = aT[:, 32:40], aT_r[32:40]
            P.op("sp", lambda e: e.dma_start(out=gfin[:], in_=gfin_d[:, :]), writes=[gfin_r], dma=True, key=gfin_r)
            wpv = w_pool.rearrange("g (k p) d -> p (g k) d", p=128)
            wbav = w_ba.rearrange("(k p) c -> p k c", p=128)
            wbpv = w_bp.rearrange("(k p) c -> p k c", p=128)
            wov = w_out.rearrange("(k p) c -> p k c", p=128)
            sched = []
            for gi in range(8):
                for mb in range(4):
                    for hf in range(2):
                        c0 = mb * 512 + hf * 256
                        sched.append(("c%dga%d_%d" % (gi, mb, hf), w_in_v[:, :, 4096 + c0:4096 + c0 + 256], 16, 256))
                        sched.append(("c%dgp%d_%d" % (gi, mb, hf), w_in_v[:, :, 6144 + c0:6144 + c0 + 256], 16, 256))
                        if mb == 0 and hf == 0:
                            sched.append(("c%dpool" % gi, wpv, 8, 256))
                        sched.append(("c%dbb%d_%d" % (gi, mb, hf),
                                      [(wbav[:, :, c0:c0 + 256], 8, 256), (wbpv[:, :, c0:c0 + 256], 8, 256)], None, None))
                for c in range(8):
                    sched.append(("c%dwo%d" % (gi, c), wov[:, :, c * 256:(c + 1) * 256], 16, 256))
                sched += ffn_sched(w2g, w2u, w2d, "c%d" % gi)
            ws = WStream(P, ring, ring_r, sched, ngroups=8, cache=WCC)
            ws._pump()

            def c_tok0(gi):
                return (0 if gi < 4 else 2048 + 256) + (gi % 4) * 512

            def load_u(gi):
                t0 = c_tok0(gi)
                P.op("sp", lambda e: e.dma_start(out=uT[:], in_=U2T[:, t0:t0 + 512].rearrange("(k p) t -> p k t", p=128)),
                     writes=uT_r, dma=True, key=uT_r[0])
            def c_geom(gi):
                return gi // 4, (gi % 4) * 512, c_tok0(gi)

            def mixer_loads_early(gi):
                s, o, tok0 = c_geom(gi)
                lo_c, hi_c = tok0 - 8, tok0 + 520
                if s == 0:
                    lo_v, hi_v = max(lo_c, 0), min(hi_c, 2048)
                else:
                    lo_v, hi_v = lo_c, hi_c
                if lo_v > lo_c:
                    P.op("pool", lambda e, n=lo_v - lo_c: e.memset(zp[:, :, 0:n], 0.0), writes=[zp_r])
                if hi_v < hi_c:
                    P.op("pool", lambda e, n=hi_c - hi_v: e.memset(zp[:, :, 528 - n:528], 0.0), writes=[zp_r])
                P.op("sp", lambda e, lo_v=lo_v, hi_v=hi_v, lo_c=lo_c: e.dma_start(
                    out=zp[:, :, lo_v - lo_c:hi_v - lo_c], in_=ZP[:, lo_v:hi_v].rearrange("(c p) t -> p c t", p=128)),
                    writes=[zp_r], dma=True, key=zp_r)
                P.op("sp", lambda e, s=s, o=o: e.dma_start(
                    out=invc[:], in_=invc_d[s:s + 1, :, o:o + 512].to_broadcast([128, 4, 512])),
                    writes=[invc_r], dma=True, key=invc_r)

            def mixer_pre(gi):
                s, o, tok0 = c_geom(gi)
                P.op("sp", lambda e, s=s, o=o: e.dma_start(
                    out=yaT, in_=YATT[:, s * 2048 + o:s * 2048 + o + 512].rearrange("(c p) t -> p c t", p=128)),
                    writes=list(yaT_r), dma=True, key=yaT_r[0])
                for gp in range(4):
                    zz = zp[:, 2 * gp:2 * gp + 2, :]
                    add = lambda e, o_, a, b: e.tensor_tensor(out=o_, in0=a, in1=b, op=ALU.add)
                    if gp == 0:
                        P.op("dve", lambda e, zz=zz: add(e, tB[:, :, 8:520], zz[:, :, 7:519], zz[:, :, 8:520]),
                             reads=[zp_r], writes=[tB_r])
                        wsum = tB
                    else:
                        P.op("dve", lambda e, zz=zz: add(e, tA[:, :, 1:527], zz[:, :, 0:526], zz[:, :, 1:527]),
                             reads=[zp_r], writes=[tA_r])
                        if gp == 1:
                            P.op("dve", lambda e: add(e, tB[:, :, 8:520], tA[:, :, 7:519], tA[:, :, 9:521]),
                                 reads=[tA_r], writes=[tB_r])
                            wsum = tB
                        else:
                            P.op("dve", lambda e: add(e, tB[:, :, 2:526], tA[:, :, 1:525], tA[:, :, 3:527]),
                                 reads=[tA_r], writes=[tB_r])
                            if gp == 2:
                                P.op("dve", lambda e: add(e, tA[:, :, 8:520], tB[:, :, 6:518], tB[:, :, 10:522]),
                                     reads=[tB_r], writes=[tA_r])
                                wsum = tA
                            else:
                                P.op("dve", lambda e: add(e, tA[:, :, 4:524], tB[:, :, 2:522], tB[:, :, 6:526]),
                                     reads=[tB_r], writes=[tA_r])
                                P.op("dve", lambda e: add(e, tB[:, :, 8:520], tA[:, :, 4:516], tA[:, :, 12:524]),
                                     reads=[tA_r], writes=[tB_r])
                                wsum = tB
                    w_res = tB_r if wsum is tB else tA_r
                    P.op("dve", lambda e, wsum=wsum, gp=gp: e.tensor_tensor(
                        out=wsum[:, :, 8:520], in0=wsum[:, :, 8:520],
                        in1=invc[:, gp:gp + 1, :].to_broadcast([128, 2, 512]), op=ALU.mult),
                        reads=[w_res, invc_r], writes=[w_res])
                    P.op("dve", lambda e, wsum=wsum, gp=gp, zz=zz: e.tensor_tensor(
                        out=pl[:, 2 * gp:2 * gp + 2, :], in0=wsum[:, :, 8:520], in1=zz[:, :, 8:520], op=ALU.subtract),
                        reads=[w_res, zp_r], writes=list(pl_r[2 * gp:2 * gp + 2]))

            def final_norm_tile(gi, t):
                s_, o_, _ = c_geom(gi)
                rsv, rs_res, b = rms_stats(xs[:, t, :], xs_r[t])
                P.op("dve", lambda e, rsv=rsv: e.scalar_tensor_tensor(
                    out=xs[:, t, :], in0=xs[:, t, :], scalar=rsv, in1=gfin[:], op0=ALU.mult, op1=ALU.mult),
                    reads=[xs_r[t], rs_res, gfin_r], writes=[xs_r[t]])
                r0 = s_ * 2048 + o_ + t * 128
                P.op("sp", lambda e, r0=r0: e.dma_start(out=y_out[r0:r0 + 128, :], in_=xs[:, t, :]),
                     reads=[xs_r[t]], dma=True, key=xs_r[t])

            def load_h1_tile(gi, t):
                r0 = c_tok0(gi) + t * 128
                P.op("sp", lambda e, r0=r0: e.dma_start(out=xs[:, t, :], in_=H1[r0:r0 + 128, :]),
                     writes=[xs_r[t]], dma=True, key=xs_r[t])

            mixer_loads_early(0)
            mixer_pre(0)
            for gi in range(8):
                s = gi // 4
                o = (gi % 4) * 512
                own_base = 0 if s == 0 else 2048 + 256
                tok0 = own_base + o
                if gi == 0:
                    load_u(0)
                    for t in range(4):
                        load_h1_tile(0, t)
                def pool_mix(gi=gi):
                    wpl, wpl_r, ipl = ws.get("c%dpool" % gi)
                    for gp in range(4):
                        for oc in range(2):
                            po, po_r = fbank()

                            def mmp(e, po=po, gp=gp, oc=oc, wpl=wpl):
                                for k in range(2):
                                    i = e.matmul(po, lhsT=wpl[:, gp * 2 + k, oc * 128:(oc + 1) * 128], rhs=pl[:, gp * 2 + k, :],
                                                 start=(k == 0), stop=(k == 1))
                                return i
                            P.op("pe", mmp, reads=list(pl_r[2 * gp:2 * gp + 2]) + [wpl_r], writes=[po_r])
                            ci = gp * 2 + oc
                            P.op("act", lambda e, po=po, ci=ci: e.activation(out=pmT[:, ci, :], in_=po, func=AF.Copy,
                                                                             scale=psT[:, ci:ci + 1]),
                                 reads=[po_r, psT_r], writes=[pmT_r[ci]])
                    ws.done(ipl)
                for mb in range(4):
                    for hf in range(2):
                        wga, wga_r, iga = ws.get("c%dga%d_%d" % (gi, mb, hf))
                        wgp, wgp_r, igp = ws.get("c%dgp%d_%d" % (gi, mb, hf))
                        first = (mb == 0 and hf == 0)
                        if not first:
                            (wba_, wbp_), wbb_r, ibb = ws.get("c%dbb%d_%d" % (gi, mb, hf))
                        for j2 in range(2):
                            m = mb * 4 + hf * 2 + j2
                            (p0, p0_r), (p1, p1_r) = fbank(), fbank()

                            def mm16(e, w, o, j2=j2):
                                for k in range(16):
                                    i = e.matmul(o, lhsT=w[:, k, j2 * 128:(j2 + 1) * 128], rhs=uT[:, k, :],
                                                 start=(k == 0), stop=(k == 15))
                                return i

                            def mm8(e, w, o, src, j2=j2):
                                for k in range(8):
                                    i = e.matmul(o, lhsT=w[:, k, j2 * 128:(j2 + 1) * 128], rhs=src[:, k, :],
                                                 start=(k == 0), stop=(k == 7))
                                return i
                            P.op("pe", lambda e, w=wga, o=p0, f=mm16: f(e, w, o), reads=uT_r + [wga_r], writes=[p0_r])
                            P.op("pe", lambda e, w=wgp, o=p1, f=mm16: f(e, w, o), reads=uT_r + [wgp_r], writes=[p1_r])
                            P.op("act", lambda e, p0=p0: e.activation(out=sg[:, 0, :], in_=p0, func=AF.Sigmoid),
                                 reads=[p0_r], writes=[sg_r[0]])
                            P.op("act", lambda e, p1=p1: e.activation(out=sg[:, 1, :], in_=p1, func=AF.Sigmoid),
                                 reads=[p1_r], writes=[sg_r[1]])
                            if first and j2 == 0:
                                pool_mix()
                                (wba_, wbp_), wbb_r, ibb = ws.get("c%dbb%d_%d" % (gi, mb, hf))
                            (p2, p2_r), (p3, p3_r) = fbank(), fbank()
                            P.op("pe", lambda e, w=wba_, o=p2, f=mm8: f(e, w, o, yaT), reads=list(yaT_r) + [wbb_r], writes=[p2_r])
                            P.op("pe", lambda e, w=wbp_, o=p3, f=mm8: f(e, w, o, pmT), reads=list(pmT_r) + [wbb_r], writes=[p3_r])
                            P.op("dve", lambda e, p2=p2: e.tensor_tensor(out=t1[:, 0, :], in0=sg[:, 0, :], in1=p2, op=ALU.mult),
                                 reads=[sg_r[0], p2_r], writes=[t1_r[0]])
                            P.op("dve", lambda e, p3=p3: e.tensor_tensor(out=t1[:, 1, :], in0=sg[:, 1, :], in1=p3, op=ALU.mult),
                                 reads=[sg_r[1], p3_r], writes=[t1_r[1]])
                            P.op("dve", lambda e, m=m: e.tensor_tensor(out=mT[:, m, :], in0=t1[:, 0, :], in1=t1[:, 1, :], op=ALU.add),
                                 reads=t1_r, writes=[mT_r[m]])
                            if gi > 0 and m in (1, 3, 5, 7):
                                final_norm_tile(gi - 1, m // 2)
                                load_h1_tile(gi, m // 2)
                        ws.done(iga)
                        ws.done(igp)
                        ws.done(ibb)
                P.op("dve", lambda e: e.memset(ssq[:, 0:32], 0.0), writes=[ssq_r])
                for c in range(8):
                    wo_, wo_r, iwo = ws.get("c%dwo%d" % (gi, c))
                    for t in range(4):
                        po, po_r = fbank()

                        def mmo(e, po=po, t=t, wo_=wo_):
                            for k in range(16):
                                i = e.matmul(po[:, 0:256], lhsT=mT[:, k, t * 128:(t + 1) * 128], rhs=wo_[:, k, :],
                                             start=(k == 0), stop=(k == 15))
                            return i
                        P.op("pe", mmo, reads=list(mT_r) + [wo_r], writes=[po_r])
                        P.op("dve", lambda e, po=po, t=t, c=c: e.tensor_tensor(
                            out=xs[:, t, c * 256:(c + 1) * 256], in0=po[:, 0:256], in1=xs[:, t, c * 256:(c + 1) * 256], op=ALU.add),
                            reads=[po_r, xs_r[t]], writes=[xs_r[t]])
                        partial_sq(xs[:, t, :], xs_r[t], c * 256, (c + 1) * 256, t * 8 + c)
                    ws.done(iwo)
                norm_group(hs_tiles, 2, uT, uT_r, pre=8)
                ffn(ws, "c%d" % gi, aT, aT_r, hs_tiles,
                    mid_hook=(lambda gi=gi: (load_u(gi + 1), mixer_loads_early(gi + 1))) if gi + 1 < 8 else None)
                if gi + 1 < 8:
                    mixer_pre(gi + 1)
                if gi == 7:
                    for t in range(4):
                        final_norm_tile(gi, t)
            P.barrier()
            P.emit()
        P.final_wait()
    return nc


def _bias_tables(rpb, interior=False):
    rpb = np.asarray(rpb, np.float32).reshape(16, 15, 31)
    T = np.full((8, 2, 64, 2, 22, 64), NEG, np.float32)
    qc = np.arange(64)
    cs = np.clip(qc - 8, 0, 48)
    kc = np.arange(64)
    colok = (kc[None, :] >= cs[:, None]) & (kc[None, :] < cs[:, None] + 16)
    dc = np.clip(kc[None, :] - qc[:, None] + 15, 0, 30)
    for qrl in range(2):
        for i in range(22):
            dr = i - qrl - 3
            if (3 <= dr <= 10) if interior else (0 <= dr <= 14):
                vals = rpb[:, dr, :][:, dc]
                vals = np.where(colok[None], vals, np.float32(NEG))
                T[:, qrl, :, :, i, :] = vals.reshape(8, 2, 64, 64).transpose(0, 2, 1, 3)
    return np.ascontiguousarray(T.reshape(8, 128, 2 * 22 * 64))


def _rowmask(core):
    rm = np.full((2, 12, 2048), NEG, np.float32)
    for s in range(2):
        L = 32 if s == 0 else 40
        own0 = 0 if s == 0 else 4
        for qt in range(16):
            r = own0 + 2 * qt
            nbr, kb = _band(s, qt)
            for qrl in range(2):
                if s == 0:
                    Rg, rows, koff = r + qrl, 32, 0
                else:
                    Rg, rows, koff = 32 * core - 4 + r + qrl, 256, 32 * core - 4
                rs = min(max(Rg - 4, 0), rows - 8)
                for krl in range(nbr):
                    kg = koff + kb + krl
                    if rs <= kg < rs + 8:
                        rm[s, krl, qt * 128 + qrl * 64: qt * 128 + qrl * 64 + 64] = 0.0
    return rm


def _invcnt(core):
    out = np.zeros((2, 4, 2048), np.float32)
    for s in range(2):
        T = 2048 if s == 0 else 16384
        t = np.arange(2048) + (0 if s == 0 else core * 2048)
        for gi, w in enumerate((2, 4, 8, 16)):
            lo = np.clip(t - w // 2, 0, T)
            hi = np.clip(t + w // 2, 0, T)
            out[s, gi] = 1.0 / (hi - lo).astype(np.float32)
    return out


_NC_CACHE = {}


def _prep(x_prompt, x_sample, g_ffn1, w1_gate, w1_up, w1_down, g_mix, w_in, rpb, w_pool,
          pool_scale, w_branch_attn, w_branch_pool, w_out, g_ffn2, w2_gate, w2_up, w2_down, g_final,
          cores=range(NCORES)):
    f = lambda a: np.ascontiguousarray(np.asarray(a, dtype=np.float32))
    x_prompt, x_sample = f(x_prompt), f(x_sample)
    fm = lambda g: f(g).reshape(-1, 128).T
    gT = np.ascontiguousarray(np.concatenate([fm(g_ffn1), fm(g_mix), fm(g_ffn2)], axis=1))
    psT = np.ascontiguousarray(fm(pool_scale))
    gfin = np.ascontiguousarray(np.broadcast_to(f(g_final).reshape(1, D), (128, D)))
    bias_t = _bias_tables(rpb)
    bias_i = _bias_tables(rpb, interior=True)
    e12 = np.zeros((12, 12, 64), np.float32)
    e12[np.arange(12), np.arange(12), :] = 1.0
    e12 = e12.reshape(12, 768)
    shared = {
        "w1_gate": f(w1_gate).reshape(D, DFF), "w1_up": f(w1_up).reshape(D, DFF), "w1_down": f(w1_down).reshape(DFF, D),
        "w2_gate": f(w2_gate).reshape(D, DFF), "w2_up": f(w2_up).reshape(D, DFF), "w2_down": f(w2_down).reshape(DFF, D),
        "w_in": f(w_in).reshape(D, 8192), "w_pool": f(w_pool).reshape(4, 256, 256),
        "w_branch_attn": f(w_branch_attn).reshape(1024, D), "w_branch_pool": f(w_branch_pool).reshape(1024, D),
        "w_out": f(w_out).reshape(D, D), "gT": gT, "psT": psT, "gfin": gfin, "bias_t": bias_t, "bias_i": bias_i, "e12": e12,
    }
    xs_pad = np.zeros((256 * 64 + 8 * 64, D), np.float32)
    xs_pad[256:256 + 16384] = x_sample[0]
    in_maps = []
    for c in cores:
        x_all = np.concatenate([x_prompt[c], xs_pad[c * 2048: c * 2048 + 2560]], axis=0)
        m = dict(shared)
        m["x_all"] = np.ascontiguousarray(x_all)
        m["rowmask"] = _rowmask(c)
        m["invcnt"] = _invcnt(c)
        in_maps.append(m)
    return in_maps


def kernel(**inputs):
    in_maps = _prep(**inputs)
    if "nc" not in _NC_CACHE:
        _NC_CACHE["nc"] = build_nc()
    res = run_bass_kernel_spmd(_NC_CACHE["nc"], in_maps, core_ids=list(range(NCORES)))
    y_prompt = np.empty((8, 2048, D), np.float32)
    y_sample = np.empty((1, 16384, D), np.float32)
    for c in range(NCORES):
        y = res.results[c]["y"]
        y_prompt[c] = y[:2048]
        y_sample[0, c * 2048:(c + 1) * 2048] = y[2048:]
    return (y_prompt, y_sample)
```
